# Optimizing a Trainium2 kernel written in Bass

```python
import math
import jax, jax.numpy as jnp
from jax import lax
import numpy as np

D_MODEL = 1024
BATCH = 8
SEQ = 2048
DEPTH = 2

N_A_LAYERS = DEPTH // 2
N_B_LAYERS = DEPTH - N_A_LAYERS
D_FF = 2816
LRU_WIDTH = D_MODEL
LRU_BLOCKS = 4
LRU_BLOCK_WIDTH = LRU_WIDTH // LRU_BLOCKS
CONV_WIDTH = 4
RG_C = 8.0
N_HEADS = 8
QK_DIM = 64
V_DIM = 2 * QK_DIM
KV_WIDTH = N_HEADS * (2 * QK_DIM + V_DIM)
ROPE_THETA = 10000.0
Q_BLOCK = 128
EPS = 1e-6

kernel_name = "yoco_hawk_diffattn_macaron"


def rmsnorm(x, g):
    xf = x.astype(jnp.float32)
    y = xf * lax.rsqrt(jnp.mean(xf * xf, axis=-1, keepdims=True) + EPS)
    return (y * g.astype(jnp.float32)).astype(x.dtype)


def swiglu_ffn(u, w_in, w_out):
    gate, up = jnp.split(u @ w_in, 2, axis=-1)
    return (jax.nn.silu(gate) * up) @ w_out


def rope_tables(seq_len):
    pos = jnp.arange(seq_len, dtype=jnp.float32)
    inv_freq = ROPE_THETA ** (-jnp.arange(0, QK_DIM, 2, dtype=jnp.float32) / QK_DIM)
    ang = pos[:, None] * inv_freq[None, :]
    return jnp.cos(ang), jnp.sin(ang)


def apply_rope(t, cos, sin):
    tf = t.astype(jnp.float32)
    t1, t2 = jnp.split(tf, 2, axis=-1)
    c = cos[None, :, None, :]
    s = sin[None, :, None, :]
    out = jnp.concatenate([t1 * c - t2 * s, t2 * c + t1 * s], axis=-1)
    return out.astype(t.dtype)


def causal_depthwise_conv(x, w, b):
    seq_len = x.shape[1]
    xp = jnp.pad(x, ((0, 0), (CONV_WIDTH - 1, 0), (0, 0)))
    out = b
    for k in range(CONV_WIDTH):
        out = out + xp[:, k:k + seq_len, :] * w[k]
    return out


def _lin_rec_combine(e1, e2):
    a1, b1 = e1
    a2, b2 = e2
    return a1 * a2, a2 * b1 + b2


def rglru_block(u, w_in, b_in, conv_w, conv_b, gate_w, gate_b, lam, w_out, b_out):
    bsz, seq_len, _ = u.shape
    y = u @ w_in + b_in
    gate_branch, xb = jnp.split(y, 2, axis=-1)
    gate_branch = jax.nn.gelu(gate_branch, approximate=True)
    xb = causal_depthwise_conv(xb, conv_w, conv_b)
    xblk = xb.reshape(bsz, seq_len, LRU_BLOCKS, LRU_BLOCK_WIDTH)
    g = jnp.einsum('bsnc,ncg->bsng', xblk, gate_w) + gate_b
    g = jax.nn.sigmoid(g.astype(jnp.float32))
    gate_x = g[..., :LRU_BLOCK_WIDTH].reshape(bsz, seq_len, LRU_WIDTH)
    gate_a = g[..., LRU_BLOCK_WIDTH:].reshape(bsz, seq_len, LRU_WIDTH)
    log_a = RG_C * gate_a * jax.nn.log_sigmoid(lam.astype(jnp.float32))
    a = jnp.exp(log_a)
    mult = jnp.sqrt(-jnp.expm1(2.0 * log_a))
    b = mult * (gate_x * xb.astype(jnp.float32))
    _, h = lax.associative_scan(_lin_rec_combine, (a, b), axis=1)
    return (h.astype(u.dtype) * gate_branch) @ w_out + b_out


def shared_kv(h, kv_norm, w_kv, cos, sin):
    bsz, seq_len, _ = h.shape
    kv = (rmsnorm(h, kv_norm) @ w_kv).reshape(bsz, seq_len, N_HEADS, 2 * QK_DIM + V_DIM)
    k1 = apply_rope(kv[..., :QK_DIM], cos, sin)
    k2 = apply_rope(kv[..., QK_DIM:2 * QK_DIM], cos, sin)
    v = kv[..., 2 * QK_DIM:]
    to_bhsd = lambda t: jnp.transpose(t, (0, 2, 1, 3))
    return to_bhsd(k1), to_bhsd(k2), to_bhsd(v)


def causal_diff_attention(q1, q2, k1, k2, v, lam):
    bsz, n_heads, seq_len, _ = q1.shape
    n_blk = seq_len // Q_BLOCK
    scale = QK_DIM ** -0.5
    kpos = jnp.arange(seq_len, dtype=jnp.int32)
    neg = jnp.finfo(jnp.float32).min

    def to_blocks(q):
        q = q.reshape(bsz, n_heads, n_blk, Q_BLOCK, QK_DIM)
        return jnp.transpose(q, (2, 0, 1, 3, 4))

    def one_block(args):
        q1b, q2b, start = args
        qpos = start + jnp.arange(Q_BLOCK, dtype=jnp.int32)
        mask = kpos[None, :] <= qpos[:, None]
        s1 = jnp.einsum('bhqd,bhkd->bhqk', q1b, k1).astype(jnp.float32) * scale
        s2 = jnp.einsum('bhqd,bhkd->bhqk', q2b, k2).astype(jnp.float32) * scale
        p1 = jax.nn.softmax(jnp.where(mask, s1, neg), axis=-1)
        p2 = jax.nn.softmax(jnp.where(mask, s2, neg), axis=-1)
        p = (p1 - lam * p2).astype(v.dtype)
        return jnp.einsum('bhqk,bhkd->bhqd', p, v)

    starts = jnp.arange(n_blk, dtype=jnp.int32) * Q_BLOCK
    out = lax.map(one_block, (to_blocks(q1), to_blocks(q2), starts))
    out = jnp.transpose(out, (1, 0, 3, 2, 4))
    return out.reshape(bsz, seq_len, n_heads, V_DIM)


def diff_attn_layer(u, k1, k2, v, w_q, lam_params, subln_g, w_o, lambda_init, cos, sin):
    bsz, seq_len, _ = u.shape
    q = (u @ w_q).reshape(bsz, seq_len, N_HEADS, 2, QK_DIM)
    q1 = jnp.transpose(apply_rope(q[..., 0, :], cos, sin), (0, 2, 1, 3))
    q2 = jnp.transpose(apply_rope(q[..., 1, :], cos, sin), (0, 2, 1, 3))
    lp = lam_params.astype(jnp.float32)
    lam = jnp.exp(jnp.sum(lp[0] * lp[1])) - jnp.exp(jnp.sum(lp[2] * lp[3])) + lambda_init
    o = causal_diff_attention(q1, q2, k1, k2, v, lam)
    o = rmsnorm(o, subln_g) * (1.0 - lambda_init)
    return o.reshape(bsz, seq_len, N_HEADS * V_DIM) @ w_o


def setup_inputs(seed: int = 0) -> dict:
    key = jax.random.key(seed)
    ks = jax.random.split(key, 32)
    f32 = jnp.float32
    nrm = lambda k, shape, fan_in: jax.random.normal(k, shape, f32) * (fan_in ** -0.5)
    gain = lambda k, shape: 1.0 + 0.02 * jax.random.normal(k, shape, f32)
    small = lambda k, shape: 0.01 * jax.random.normal(k, shape, f32)
    R = LRU_WIDTH
    u = jax.random.uniform(ks[14], (N_A_LAYERS, R), f32, 0.9, 0.999)
    a0 = u ** (1.0 / RG_C)
    rg_lambda = jnp.log(a0) - jnp.log1p(-a0)
    return {
        "x": jax.random.normal(ks[0], (BATCH, SEQ, D_MODEL), f32),
        "ffn1_norm": gain(ks[1], (DEPTH, D_MODEL)),
        "ffn1_w_in": nrm(ks[2], (DEPTH, D_MODEL, 2 * D_FF), D_MODEL),
        "ffn1_w_out": nrm(ks[3], (DEPTH, D_FF, D_MODEL), D_FF),
        "mix_norm": gain(ks[4], (DEPTH, D_MODEL)),
        "ffn2_norm": gain(ks[5], (DEPTH, D_MODEL)),
        "ffn2_w_in": nrm(ks[6], (DEPTH, D_MODEL, 2 * D_FF), D_MODEL),
        "ffn2_w_out": nrm(ks[7], (DEPTH, D_FF, D_MODEL), D_FF),
        "rg_w_in": nrm(ks[8], (N_A_LAYERS, D_MODEL, 2 * R), D_MODEL),
        "rg_b_in": small(ks[9], (N_A_LAYERS, 2 * R)),
        "rg_conv_w": nrm(ks[10], (N_A_LAYERS, CONV_WIDTH, R), CONV_WIDTH),
        "rg_conv_b": small(ks[11], (N_A_LAYERS, R)),
        "rg_gate_w": nrm(ks[12], (N_A_LAYERS, LRU_BLOCKS, LRU_BLOCK_WIDTH, 2 * LRU_BLOCK_WIDTH), LRU_BLOCK_WIDTH),
        "rg_gate_b": small(ks[13], (N_A_LAYERS, LRU_BLOCKS, 2 * LRU_BLOCK_WIDTH)),
        "rg_lambda": rg_lambda,
        "rg_w_out": nrm(ks[15], (N_A_LAYERS, R, D_MODEL), R),
        "rg_b_out": small(ks[16], (N_A_LAYERS, D_MODEL)),
        "kv_norm": gain(ks[17], (D_MODEL,)),
        "w_kv": nrm(ks[18], (D_MODEL, KV_WIDTH), D_MODEL),
        "diff_w_q": nrm(ks[19], (N_B_LAYERS, D_MODEL, N_HEADS * 2 * QK_DIM), D_MODEL),
        "diff_lambda": 0.1 * jax.random.normal(ks[20], (N_B_LAYERS, 4, QK_DIM), f32),
        "diff_subln": gain(ks[21], (N_B_LAYERS, V_DIM)),
        "diff_w_o": nrm(ks[22], (N_B_LAYERS, N_HEADS * V_DIM, D_MODEL), N_HEADS * V_DIM),
        "final_norm": gain(ks[23], (D_MODEL,)),
    }


def reference(x, ffn1_norm, ffn1_w_in, ffn1_w_out, mix_norm, ffn2_norm, ffn2_w_in, ffn2_w_out,
              rg_w_in, rg_b_in, rg_conv_w, rg_conv_b, rg_gate_w, rg_gate_b, rg_lambda, rg_w_out, rg_b_out,
              kv_norm, w_kv, diff_w_q, diff_lambda, diff_subln, diff_w_o, final_norm):
    seq_len = x.shape[1]
    cos, sin = rope_tables(seq_len)
    h = x
    k1 = k2 = v = None
    for l in range(DEPTH):
        if l == N_A_LAYERS:
            k1, k2, v = shared_kv(h, kv_norm, w_kv, cos, sin)
        h = h + 0.5 * swiglu_ffn(rmsnorm(h, ffn1_norm[l]), ffn1_w_in[l], ffn1_w_out[l])
        u = rmsnorm(h, mix_norm[l])
        if l < N_A_LAYERS:
            h = h + rglru_block(u, rg_w_in[l], rg_b_in[l], rg_conv_w[l], rg_conv_b[l],
                                rg_gate_w[l], rg_gate_b[l], rg_lambda[l], rg_w_out[l], rg_b_out[l])
        else:
            j = l - N_A_LAYERS
            lambda_init = 0.8 - 0.6 * math.exp(-0.3 * l)
            h = h + diff_attn_layer(u, k1, k2, v, diff_w_q[j], diff_lambda[j], diff_subln[j],
                                    diff_w_o[j], lambda_init, cos, sin)
        h = h + 0.5 * swiglu_ffn(rmsnorm(h, ffn2_norm[l]), ffn2_w_in[l], ffn2_w_out[l])
    return rmsnorm(h, final_norm)
```

```python
import math
from contextlib import ExitStack

import numpy as np
import ml_dtypes
import concourse.bass as bass
import concourse.mybir as mybir
from concourse.bass_utils import run_bass_kernel_spmd

F32 = mybir.dt.float32
BF16 = mybir.dt.bfloat16
AF = mybir.ActivationFunctionType
ALU = mybir.AluOpType

T = 2048
D = 1024
KD = 8
FF = 2816
NFF = 22
NB = 512
NTB = 4
EPS = 1e-6
GRAN = 64
LAMBDA_INIT = 0.8 - 0.6 * math.exp(-0.3 * 1)

ALL_STAGES = ("ffn1_0", "rg", "ffn2_0", "ffn1_1", "attn", "ffn2_1")

CV = {}
_c = 0
for _name, _n in (("ffn1_norm0", 8), ("ffn1_norm1", 8), ("mix_norm0", 8), ("mix_norm1", 8),
                  ("ffn2_norm0", 8), ("ffn2_norm1", 8), ("kv_norm", 8), ("final_norm", 8),
                  ("rg_b_in", 16), ("rg_conv_w", 32), ("rg_conv_b", 8), ("rg_gate_b", 16),
                  ("rg_lambda", 8), ("rg_b_out", 8), ("subln", 1), ("dlam", 256)):
    CV[_name] = _c
    _c += _n
NCV = _c
DV = {}
_c = 0
for _name, _n in (("c8", 8), ("c16", 8), ("ngb", 16), ("lam_s", 2), ("lam_e", 2), ("nlam", 1),
                  ("gsub", 1), ("tmp", 8), ("hist", 24), ("hst", 8), ("lp", 128)):
    DV[_name] = _c
    _c += _n
NDV = _c


class Arena:
    def __init__(self, name, nbytes):
        n = (nbytes + GRAN - 1) // GRAN + 1
        self.name = name
        self.last_w = [None] * n
        self.readers = [None] * n


class Ref:
    __slots__ = ("ap", "rngs")

    def __init__(self, ap, rngs):
        self.ap = ap
        self.rngs = rngs


class Op:
    __slots__ = ("eng", "fn", "deps", "inc", "semval", "dma_sem", "dma_val", "seq")


class Buf:
    def __init__(self, arena, ap_f32, byte_off, dt, dims):
        self.arena = arena
        self.off = byte_off
        self.dt = dt
        self.dims = tuple(dims)
        self.es = 4 if dt == F32 else 2
        n = int(np.prod(dims))
        assert byte_off % 4 == 0 and (n * self.es) % 4 == 0
        ap = ap_f32[:, byte_off // 4:(byte_off + n * self.es) // 4]
        if dt != F32:
            ap = ap.bitcast(dt)
        if len(dims) == 2:
            ap = ap.rearrange("p (a b) -> p a b", a=dims[0])
        elif len(dims) == 3:
            ap = ap.rearrange("p (a b c) -> p a b c", a=dims[0], b=dims[1])
        elif len(dims) == 4:
            ap = ap.rearrange("p (a b c d) -> p a b c d", a=dims[0], b=dims[1], c=dims[2])
        self.full = ap
        self.nbytes = n * self.es

    def __getitem__(self, ix):
        if not isinstance(ix, tuple):
            ix = (ix,)
        ap = self.full[ix]
        fix = list(ix[1:]) + [slice(None)] * (len(self.dims) - len(ix) + 1)
        bounds = []
        for d, i in zip(self.dims, fix):
            if isinstance(i, int):
                bounds.append((i, i + 1))
            else:
                assert i.step is None
                bounds.append((0 if i.start is None else i.start, d if i.stop is None else i.stop))
        k = len(self.dims)
        strides = [1] * k
        for i in range(k - 2, -1, -1):
            strides[i] = strides[i + 1] * self.dims[i + 1]
        j = k - 1
        while j > 0 and bounds[j] == (0, self.dims[j]):
            j -= 1
        run_len = (bounds[j][1] - bounds[j][0]) * strides[j]
        outs = [bounds[j][0] * strides[j]]
        for i in range(j - 1, -1, -1):
            outs = [o + t * strides[i] for t in range(*bounds[i]) for o in outs]
        if len(outs) > 64:
            lo = min(outs)
            hi = max(outs) + run_len
            rngs = [(self.arena, self.off + lo * self.es, self.off + hi * self.es)]
        else:
            rngs = [(self.arena, self.off + o * self.es, self.off + (o + run_len) * self.es) for o in outs]
        return Ref(ap, rngs)


class Prog:
    ENGS = ("pe", "act", "dve", "pool", "sp")

    def __init__(self):
        self.ops = {e: [] for e in self.ENGS}
        self.dma_count = {}
        self.seq = 0

    def add(self, eng, fn, reads=(), writes=(), dma=None):
        op = Op()
        op.eng = eng
        op.fn = fn
        op.inc = False
        op.semval = None
        op.dma_sem = dma
        op.dma_val = None
        self.seq += 1
        op.seq = self.seq
        deps = {}

        def add_dep(o):
            if o is None:
                return
            if o.dma_sem is not None:
                key = ("dma", id(o))
            else:
                if o.eng == eng and eng == "pe":
                    return
                key = o.eng
            cur = deps.get(key)
            if cur is None or o.seq > cur.seq:
                deps[key] = o

        for r in reads:
            for (ar, lo, hi) in r.rngs:
                for g in range(lo // GRAN, (hi - 1) // GRAN + 1):
                    add_dep(ar.last_w[g])
        for w in writes:
            for (ar, lo, hi) in w.rngs:
                for g in range(lo // GRAN, (hi - 1) // GRAN + 1):
                    add_dep(ar.last_w[g])
                    rd = ar.readers[g]
                    if rd:
                        for o in rd.values():
                            add_dep(o)
        rkey = eng if dma is None else ("dma", id(op))
        for r in reads:
            for (ar, lo, hi) in r.rngs:
                for g in range(lo // GRAN, (hi - 1) // GRAN + 1):
                    rd = ar.readers[g]
                    if rd is None:
                        rd = ar.readers[g] = {}
                    rd[rkey] = op
        for w in writes:
            for (ar, lo, hi) in w.rngs:
                for g in range(lo // GRAN, (hi - 1) // GRAN + 1):
                    ar.last_w[g] = op
                    ar.readers[g] = None
        op.deps = list(deps.values())
        if dma is not None:
            c = self.dma_count.get(dma, 0) + 1
            self.dma_count[dma] = c
            op.dma_val = 16 * c
        self.ops[eng].append(op)
        return op

    def emit(self, nc):
        for e in self.ENGS:
            for op in self.ops[e]:
                for d in op.deps:
                    if d.dma_sem is None:
                        d.inc = True
        for e in self.ENGS:
            c = 0
            for op in self.ops[e]:
                if op.inc and op.dma_sem is None:
                    c += 1
                    op.semval = c
        with ExitStack() as es:
            sems = {}
            for e in ("pe", "act", "dve", "pool"):
                sems[e] = es.enter_context(nc.semaphore("s_" + e))
            for name in self.dma_count:
                sems[name] = es.enter_context(nc.semaphore("d_" + name))
            block = es.enter_context(nc.Block())

            def run(e, engobj):
                known = {}
                for op in self.ops[e]:
                    need = {}
                    for d in op.deps:
                        if d.dma_sem is not None:
                            key, v = d.dma_sem, d.dma_val
                        else:
                            key, v = d.eng, d.semval
                        if known.get(key, 0) >= v:
                            continue
                        if need.get(key, 0) < v:
                            need[key] = v
                    for key, v in need.items():
                        engobj.wait_ge(sems[key], v)
                        known[key] = v
                    if op.fn is not None:
                        ins = op.fn(engobj)
                        if op.dma_sem is not None:
                            ins.then_inc(sems[op.dma_sem], 16)
                        elif op.inc:
                            ins.then_inc(sems[e], 1)

            @block.tensor
            def _(eng):
                run("pe", eng)

            @block.scalar
            def _(eng):
                run("act", eng)

            @block.vector
            def _(eng):
                run("dve", eng)

            @block.gpsimd
            def _(eng):
                run("pool", eng)

            @block.sync
            def _(eng):
                run("sp", eng)


def build_program(stages=ALL_STAGES, dbg=False):
    nc = bass.Bass("TRN2", target_bir_lowering=False)
    P = Prog()

    def dram_in(name, shape, dt=F32):
        return nc.dram_tensor(name, list(shape), dt, kind="ExternalInput").ap()

    xT = dram_in("xT", [D, T])
    cvec_d = dram_in("cvec", [128, NCV])
    rope_d = dram_in("rope", [128, 2 * T])
    c16_d = dram_in("c16", [128, 512], BF16)
    ffn_w_in = {"ffn1": dram_in("ffn1_w_in", [2, D, 2 * FF]), "ffn2": dram_in("ffn2_w_in", [2, D, 2 * FF])}
    ffn_w_out = {"ffn1": dram_in("ffn1_w_out", [2, FF, D]), "ffn2": dram_in("ffn2_w_out", [2, FF, D])}
    rg_w_in = dram_in("rg_w_in", [1, D, 2 * D])
    rg_gate_w = dram_in("rg_gate_w", [1, 4, 256, 512])
    rg_w_out = dram_in("rg_w_out", [1, D, D])
    w_kv = dram_in("w_kv", [D, 2048])
    w_q = dram_in("diff_w_q", [1, D, D])
    w_o = dram_in("diff_w_o", [1, D, D])
    outT = nc.dram_tensor("outT", [D, T], F32, kind="ExternalOutput").ap()
    dbg_d = None
    if dbg:
        dbg_d = nc.dram_tensor("dbg", [len(stages), D, T], F32, kind="ExternalOutput").ap()

    with ExitStack() as es:
        es.enter_context(nc.allow_low_precision("bf16 matmul operands, fp32 accumulation"))

        def sb(name, nbytes):
            h = es.enter_context(nc.sbuf_tensor(name, [128, nbytes // 4], F32))
            return Arena(name, nbytes), h[:]

        aH, hH = sb("HT", KD * T * 4)
        aN, hN = sb("XN", KD * T * 2)
        aC, hC = sb("XC", KD * T * 2)
        DBYTES = 56 * 1024
        aD, hD = sb("DD", DBYTES)
        EC_BYTES = (NCV + NDV) * 4 + 1024
        aE, hE = sb("EC", EC_BYTES)
        SBYTES = 20 * 1024
        aS, hS = sb("SS", SBYTES)

        HT = Buf(aH, hH, 0, F32, (KD, T))
        XN = Buf(aN, hN, 0, BF16, (KD, T))
        XC = Buf(aC, hC, 0, BF16, (KD, T))
        CVB = Buf(aE, hE, 0, F32, (NCV,))
        DVB = Buf(aE, hE, NCV * 4, F32, (NDV,))
        C16 = Buf(aE, hE, (NCV + NDV) * 4, BF16, (4, 128))

        psums = []
        for i in range(8):
            h = es.enter_context(nc.psum_tensor("ps%d" % i, [128, NB], F32))
            ar = Arena("ps%d" % i, NB * 4)
            psums.append((ar, h[:]))

        def PS(i, c0=0, c1=NB, p0=0, p1=128):
            ar, ap = psums[i]
            return Ref(ap[p0:p1, c0:c1], [(ar, c0 * 4, c1 * 4)])

        def cv(name, c=0, n=1):
            o = CV[name] + c
            return CVB[:, o:o + n]

        def dv(name, c=0, n=1):
            o = DV[name] + c
            return DVB[:, o:o + n]

        ONES = C16[:, 0, :]
        ONESM = C16[:, 1, :]
        ONESV = C16[:, 2, :]
        TRI = C16[:, 3, :]

        def MM(out, lhsT, rhs, start, stop):
            P.add("pe", lambda e: e.matmul(out.ap, lhsT.ap, rhs.ap, start=start, stop=stop),
                  reads=[lhsT, rhs], writes=[out])

        def _a(x):
            return x.ap if isinstance(x, Ref) else x

        def ACT(out, in_, func, bias=0.0, scale=1.0):
            rd = [in_] + [x for x in (bias, scale) if isinstance(x, Ref)]
            P.add("act", lambda e: e.activation(out.ap, in_.ap, func, bias=_a(bias), scale=_a(scale)),
                  reads=rd, writes=[out])

        def TS(eng, out, in0, s1, s2, op0, op1=None):
            rd = [in0] + [x for x in (s1, s2) if isinstance(x, Ref)]
            if op1 is None:
                P.add(eng, lambda e: e.tensor_scalar(out.ap, in0.ap, _a(s1), None, op0), reads=rd, writes=[out])
            else:
                P.add(eng, lambda e: e.tensor_scalar(out.ap, in0.ap, _a(s1), _a(s2), op0, op1), reads=rd, writes=[out])

        def STT(eng, out, in0, s, in1, op0, op1):
            rd = [in0, in1] + ([s] if isinstance(s, Ref) else [])
            P.add(eng, lambda e: e.scalar_tensor_tensor(out.ap, in0.ap, _a(s), in1.ap, op0, op1), reads=rd, writes=[out])

        def TT(eng, out, in0, in1, op):
            P.add(eng, lambda e: e.tensor_tensor(out.ap, in0.ap, in1.ap, op), reads=[in0, in1], writes=[out])

        def SIGFIN(x):
            TS("dve", x, x, 1.0, None, ALU.add)
            P.add("dve", lambda e: e.reciprocal(x.ap, x.ap), reads=[x], writes=[x])

        def CP(eng, out, in_):
            P.add(eng, lambda e: e.tensor_copy(out.ap, in_.ap), reads=[in_], writes=[out])

        def DMA(eng, out, in_, sem, reads=(), writes=()):
            oa = out.ap if isinstance(out, Ref) else out
            ia = in_.ap if isinstance(in_, Ref) else in_
            return P.add(eng, lambda e: e.dma_start(out=oa, in_=ia), reads=reads, writes=writes, dma=sem)

        DMA("sp", CVB[:, :], cvec_d, "cvec", writes=[CVB[:, :]])
        DMA("sp", C16[:, :, :], c16_d.rearrange("p (a b) -> p a b", a=4), "c16", writes=[C16[:, :, :]])
        xT_v = xT.rearrange("(c p) t -> p c t", p=128)
        for tb in range(NTB):
            r = HT[:, :, tb * NB:(tb + 1) * NB]
            DMA("sp", r, xT_v[:, :, tb * NB:(tb + 1) * NB], "x%d" % tb, writes=[r])

        SQ = [Buf(aS, hS, i * 1024, BF16, (NB,)) for i in range(4)]
        RSTD = [Buf(aS, hS, 4096 + i * 2048, F32, (NB,)) for i in range(2)]
        SG = [Buf(aS, hS, 8192 + i * 2048, F32, (NB,)) for i in range(2)]
        cnt = {"sq": 0, "rstd": 0, "sg": 0, "pp": 0, "po": 0}

        def norm_n1(tb):
            ts = slice(tb * NB, (tb + 1) * NB)
            pn = PS(7)
            for c in range(KD):
                sq = SQ[cnt["sq"] % 4][:, :]
                cnt["sq"] += 1
                ACT(sq, HT[:, c, ts], AF.Square)
                MM(pn, ONESM, sq, c == 0, c == KD - 1)
            rstd = RSTD[tb % 2][:, :]
            ACT(rstd, pn, AF.Ln, bias=EPS)
            ACT(rstd, rstd, AF.Exp, scale=-0.5)

        def norm_n2(tb, gains_dsts):
            ts = slice(tb * NB, (tb + 1) * NB)
            rstd = RSTD[tb % 2][:, :]
            for gname, dst in gains_dsts:
                for c in range(KD):
                    STT("dve", dst(c, tb), HT[:, c, ts], cv(gname, c), rstd, ALU.mult, ALU.mult)

        def emit_norm(gains_dsts):
            for tb in range(NTB):
                norm_n1(tb)
                norm_n2(tb, gains_dsts)

        class NormTail:
            def __init__(self, gains_dsts, post=None):
                self.g = gains_dsts
                self.post = post

            def _n2(self, tb):
                norm_n2(tb, self.g)
                if self.post is not None:
                    self.post(tb)

            def after_tb(self, k):
                if k >= 2:
                    self._n2(k - 2)
                if k >= 1:
                    norm_n1(k - 1)

            def finish(self):
                self._n2(NTB - 2)
                norm_n1(NTB - 1)
                self._n2(NTB - 1)

        def dst_of(buf):
            return lambda c, tb: buf[:, c, tb * NB:(tb + 1) * NB]

        FGROUPS = [[0, 1, 2, 3, 4], [5, 6, 7, 8, 9], [10, 11, 12, 13], [14, 15, 16, 17], [18, 19, 20, 21]]
        G = Buf(aD, hD, 0, BF16, (5, T))
        WIN = [Buf(aD, hD, 20480 + i * 4096, BF16, (2, KD, 128)) for i in range(4)]
        WOUT = [Buf(aD, hD, 36864 + i * 2048, BF16, (D,)) for i in range(10)]

        def emit_ffn(which, l, tail=None):
            w_in = ffn_w_in[which][l].rearrange("(kc p) (two m n) -> p two kc m n", p=128, two=2, m=NFF)
            w_out = ffn_w_out[which][l].rearrange("(m p) n -> p m n", p=128)
            tag = which + str(l)

            def load_win(m):
                s = m % 4
                r = WIN[s][:, :, :, :]
                DMA("pool", r, w_in[:, :, :, m, :], "win%d" % s, writes=[r])

            def load_wout(m):
                s = m % 10
                r = WOUT[s][:, :]
                DMA("pool", r, w_out[:, m, :], "wout%d" % s, writes=[r])

            for m in range(3):
                load_win(m)
            for m in FGROUPS[0]:
                load_wout(m)
            for gi, grp in enumerate(FGROUPS):
                for j, m in enumerate(grp):
                    if m + 3 < NFF:
                        load_win(m + 3)
                    W = WIN[m % 4]
                    for tb in range(NTB):
                        ts = slice(tb * NB, (tb + 1) * NB)
                        pp = cnt["pp"] % 2
                        cnt["pp"] += 1
                        pg, pu = PS(2 * pp), PS(2 * pp + 1)
                        for kc in range(KD):
                            MM(pg, W[:, 0, kc, :], XN[:, kc, ts], kc == 0, kc == KD - 1)
                        for kc in range(KD):
                            MM(pu, W[:, 1, kc, :], XN[:, kc, ts], kc == 0, kc == KD - 1)
                        sg = SG[cnt["sg"] % 2][:, :]
                        cnt["sg"] += 1
                        ACT(sg, pg, AF.Silu)
                        TT("dve", G[:, j, ts], sg, pu, ALU.mult)
                if gi + 1 < len(FGROUPS):
                    for m in FGROUPS[gi + 1]:
                        load_wout(m)
                lastg = gi == len(FGROUPS) - 1
                order = ([(mo, tb) for tb in range(NTB) for mo in range(KD)] if lastg
                         else [(mo, tb) for mo in range(KD) for tb in range(NTB)])
                for (mo, tb) in order:
                    ts = slice(tb * NB, (tb + 1) * NB)
                    po = PS(4 + cnt["po"] % 2)
                    cnt["po"] += 1
                    for j, m in enumerate(grp):
                        MM(po, WOUT[m % 10][:, mo * 128:(mo + 1) * 128], G[:, j, ts], j == 0, j == len(grp) - 1)
                    STT("dve", HT[:, mo, ts], po, 0.5, HT[:, mo, ts], ALU.mult, ALU.add)
                    if lastg and tail is not None and mo == KD - 1:
                        tail.after_tb(tb)
            if tail is not None:
                tail.finish()

        def emit_rg(tail=None):
            WRI = Buf(aD, hD, 0, BF16, (KD, 2 * D))
            WRG = Buf(aD, hD, 32768, BF16, (4, 2, 512))
            WRO = Buf(aD, hD, 40960, BF16, (KD, D))
            wi = rg_w_in[0].rearrange("(kc p) n -> p kc n", p=128)
            for half in range(2):
                r = WRI[:, :, half * D:(half + 1) * D]
                DMA("pool", r, wi[:, :, half * D:(half + 1) * D], "wri%d" % half, writes=[r])
            r = WRG[:, :, :, :]
            DMA("pool", r, rg_gate_w[0].rearrange("n (k p) g -> p n k g", p=128), "wrg", writes=[r])
            r = WRO[:, :, :]
            DMA("pool", r, rg_w_out[0].rearrange("(kc p) n -> p kc n", p=128), "wro", writes=[r])

            tmp = dv("tmp", 0, 8)
            ACT(tmp, cv("rg_lambda", 0, 8), AF.Exp, scale=-1.0)
            ACT(tmp, tmp, AF.Ln, bias=1.0)
            TS("dve", dv("c8", 0, 8), tmp, -8.0, None, ALU.mult)
            TS("dve", dv("c16", 0, 8), tmp, -16.0, None, ALU.mult)
            TS("dve", dv("ngb", 0, 16), cv("rg_gate_b", 0, 16), -1.0, None, ALU.mult)
            P.add("dve", lambda e: e.memset(dv("hist", 0, 24).ap, 0.0), writes=[dv("hist", 0, 24)])
            P.add("dve", lambda e: e.memset(dv("hst", 0, 8).ap, 0.0), writes=[dv("hst", 0, 8)])

            def tb_(i, n=NB, dt=F32):
                return Buf(aC, hC, i * 2048, dt, (n,))
            HG = Buf(aC, hC, 0, BF16, (KD, NB))
            XB = Buf(aC, hC, 4 * 2048, F32, (NB + 4,))
            XCF = [[tb_(6), tb_(7)], [Buf(aS, hS, 14336, F32, (NB,)), Buf(aS, hS, 18432, F32, (NB,))]]
            XCB = [[Buf(aC, hC, 8 * 2048, BF16, (NB,)), Buf(aC, hC, 8 * 2048 + 1024, BF16, (NB,))],
                   [Buf(aC, hC, 13 * 2048, BF16, (NB,)), Buf(aC, hC, 13 * 2048 + 1024, BF16, (NB,))]]
            TSETS = [[tb_(9), tb_(10), tb_(11), tb_(12), None, tb_(14)],
                     [tb_(15), Buf(aS, hS, 8192, F32, (NB,)), Buf(aS, hS, 10240, F32, (NB,)),
                      Buf(aS, hS, 12288, F32, (NB,)), None, Buf(aS, hS, 16384, F32, (NB,))]]
            KG = math.sqrt(2.0 / math.pi)
            items = [(tb, n) for tb in range(NTB) for n in range(4)]

            def stage_a(i):
                tb, n = items[i]
                ts = slice(tb * NB, (tb + 1) * NB)
                sl = i % 2
                for jj in range(2):
                    c = 2 * n + jj
                    px = PS(0)
                    for kc in range(KD):
                        MM(px, WRI[:, kc, D + c * 128:D + (c + 1) * 128], XN[:, kc, ts], kc == 0, kc == KD - 1)
                    CP("pool", XB[:, 0:3], dv("hist", 3 * c, 3))
                    TS("dve", XB[:, 3:3 + NB], px, cv("rg_b_in", 8 + c), None, ALU.add)
                    xc = XCF[sl][jj][:, :]
                    TS("dve", xc, XB[:, 0:NB], cv("rg_conv_w", 0 * 8 + c), cv("rg_conv_b", c), ALU.mult, ALU.add)
                    for k in range(1, 4):
                        STT("dve", xc, XB[:, k:k + NB], cv("rg_conv_w", k * 8 + c), xc, ALU.mult, ALU.add)
                    CP("pool", dv("hist", 3 * c, 3), XB[:, NB:NB + 3])

            def stage_cast(i):
                sl = i % 2
                for jj in range(2):
                    ACT(XCB[sl][jj][:, :], XCF[sl][jj][:, :], AF.Copy)

            def stage_b(i, part):
                tb, n = items[i]
                ts = slice(tb * NB, (tb + 1) * NB)
                sl = i % 2

                def ctx(jj):
                    return 2 * n + jj, XCF[sl][jj][:, :], TSETS[jj]

                def st_gb_mm(jj):
                    c, xc, (B4, B5, B6, B7, B8, B9) = ctx(jj)
                    pgb = PS(3 if jj == 0 else 5)
                    for kc in range(KD):
                        MM(pgb, WRI[:, kc, c * 128:(c + 1) * 128], XN[:, kc, ts], kc == 0, kc == KD - 1)
                    ACT(B9[:, :], pgb, AF.Gelu_apprx_tanh, bias=cv("rg_b_in", c))

                def st_gates_mm(jj):
                    c, xc, (B4, B5, B6, B7, B8, B9) = ctx(jj)
                    pgx, pga = (PS(1), PS(2)) if jj == 0 else (PS(6), PS(7))
                    for k in range(2):
                        MM(pgx, WRG[:, n, k, jj * 128:(jj + 1) * 128], XCB[sl][k][:, :], k == 0, k == 1)
                    for k in range(2):
                        MM(pga, WRG[:, n, k, 256 + jj * 128:256 + (jj + 1) * 128], XCB[sl][k][:, :], k == 0, k == 1)
                    ACT(B4[:, :], pgx, AF.Exp, bias=dv("ngb", n * 4 + jj), scale=-1.0)
                    ACT(B5[:, :], pga, AF.Exp, bias=dv("ngb", n * 4 + 2 + jj), scale=-1.0)

                def st_v(jj):
                    c, xc, (B4, B5, B6, B7, B8, B9) = ctx(jj)
                    TT("pool", B9[:, :], B8[:, :], B8[:, :], ALU.mult)
                    TS("pool", B9[:, :], B9[:, :], 0.044715, 1.0, ALU.mult, ALU.add)
                    TT("pool", B9[:, :], B9[:, :], B8[:, :], ALU.mult)

                def st_r1(jj):
                    c, xc, (B4, B5, B6, B7, B8, B9) = ctx(jj)
                    ACT(B5[:, :], B5[:, :], AF.Ln, bias=1.0)

                def st_r2(jj):
                    c, xc, (B4, B5, B6, B7, B8, B9) = ctx(jj)
                    ACT(B5[:, :], B5[:, :], AF.Exp, scale=-1.0)

                def st_a(jj):
                    c, xc, (B4, B5, B6, B7, B8, B9) = ctx(jj)
                    ACT(B6[:, :], B5[:, :], AF.Exp, scale=dv("c8", c))
                    ACT(B7[:, :], B5[:, :], AF.Exp, scale=dv("c16", c))

                def st_ln(jj):
                    c, xc, (B4, B5, B6, B7, B8, B9) = ctx(jj)
                    ACT(B4[:, :], B4[:, :], AF.Ln, bias=1.0)
                    ACT(B7[:, :], B7[:, :], AF.Ln, bias=1.0, scale=-1.0)
                    STT("dve", B7[:, :], B7[:, :], 0.5, B4[:, :], ALU.mult, ALU.subtract)

                def st_m(jj):
                    c, xc, (B4, B5, B6, B7, B8, B9) = ctx(jj)
                    ACT(B7[:, :], B7[:, :], AF.Exp)
                    TT("dve", B4[:, :], B7[:, :], xc, ALU.mult)

                def st_scan(jj):
                    c, xc, (B4, B5, B6, B7, B8, B9) = ctx(jj)
                    hs = B5[:, :]
                    P.add("dve", lambda e, hs=hs, c=c, B6=B6, B4=B4: e.tensor_tensor_scan(
                        hs.ap, B6[:, :].ap, B4[:, :].ap, dv("hst", c).ap, ALU.mult, ALU.add),
                        reads=[B6[:, :], B4[:, :], dv("hst", c)], writes=[hs])
                    CP("pool", dv("hst", c), B5[:, NB - 1:NB])

                def st_g1(jj):
                    c, xc, (B4, B5, B6, B7, B8, B9) = ctx(jj)
                    ACT(B9[:, :], B9[:, :], AF.Exp, scale=-2.0 * KG)

                def st_g2(jj):
                    c, xc, (B4, B5, B6, B7, B8, B9) = ctx(jj)
                    ACT(B9[:, :], B9[:, :], AF.Ln, bias=1.0)

                def st_g3(jj):
                    c, xc, (B4, B5, B6, B7, B8, B9) = ctx(jj)
                    ACT(B9[:, :], B9[:, :], AF.Exp, scale=-1.0)

                def st_hg(jj):
                    c, xc, (B4, B5, B6, B7, B8, B9) = ctx(jj)
                    TT("pool", HG[:, c, :], B5[:, :], B9[:, :], ALU.mult)

                parts = {"b1a": (st_gb_mm, st_gates_mm, st_r1, st_r2, st_a),
                         "b1b": (st_ln,),
                         "b2": (st_m, st_scan, st_hg)}
                for st in parts[part]:
                    for jj in range(2):
                        st(jj)

            def outproj(tb):
                ts = slice(tb * NB, (tb + 1) * NB)
                for mo in range(KD):
                    po = PS(4 + cnt["po"] % 2)
                    cnt["po"] += 1
                    for kc in range(KD):
                        MM(po, WRO[:, kc, mo * 128:(mo + 1) * 128], HG[:, kc, :], kc == 0, kc == KD - 1)
                    STT("dve", HT[:, mo, ts], po, cv("rg_b_out", mo), HT[:, mo, ts], ALU.add, ALU.add)

            stage_a(0)
            stage_cast(0)
            pending = None
            for i in range(len(items)):
                stage_b(i, "b1a")
                if i + 1 < len(items):
                    stage_a(i + 1)
                if pending is not None:
                    outproj(pending)
                    if tail is not None:
                        tail.after_tb(pending)
                    pending = None
                stage_b(i, "b1b")
                stage_b(i, "b2")
                if i + 1 < len(items):
                    stage_cast(i + 1)
                if items[i][1] == 3:
                    pending = items[i][0]
            outproj(pending)
            if tail is not None:
                tail.after_tb(pending)
                tail.finish()

        def emit_attn(tail=None):
            QT = Buf(aD, hD, 0, BF16, (KD, T))
            ROPE = Buf(aD, hD, 32768, F32, (2, T))
            KT = Buf(aD, hD, 49152, BF16, (T,))
            VV = Buf(aD, hD, 53248, BF16, (T,))
            WA = Buf(aS, hS, 0, BF16, (KD, 128))
            WB = Buf(aS, hS, 2048, BF16, (KD, 128))
            WVb = Buf(aS, hS, 4096, BF16, (KD, 128))
            T1 = Buf(aS, hS, 6144, F32, (NB,))
            T2 = Buf(aS, hS, 8192, F32, (NB,))

            r = ROPE[:, :, :]
            DMA("sp", r, rope_d.rearrange("p (a t) -> p a t", a=2), "rope", writes=[r])

            TT("dve", dv("lp", 0, 64), cv("dlam", 0, 64), cv("dlam", 64, 64), ALU.mult)
            TT("dve", dv("lp", 64, 64), cv("dlam", 128, 64), cv("dlam", 192, 64), ALU.mult)
            for i in range(2):
                P.add("dve", lambda e, i=i: e.tensor_reduce(dv("lam_s", i).ap, dv("lp", 64 * i, 64).ap,
                                                            mybir.AxisListType.X, ALU.add),
                      reads=[dv("lp", 64 * i, 64)], writes=[dv("lam_s", i)])
            ACT(dv("lam_e", 0, 2), dv("lam_s", 0, 2), AF.Exp)
            TT("dve", dv("nlam"), dv("lam_e", 1), dv("lam_e", 0), ALU.subtract)
            TS("dve", dv("nlam"), dv("nlam"), -LAMBDA_INIT, None, ALU.add)
            TS("dve", dv("gsub"), cv("subln"), 1.0 - LAMBDA_INIT, None, ALU.mult)

            def load_w(dst, src_ap, sem):
                r = dst[:, :, :]
                DMA("pool", r, src_ap, sem, writes=[r])

            def swap_halves(dst, src):
                for g in range(2):
                    for h in range(2):
                        CP("pool", dst[:, :, g * 64 + (1 - h) * 32:g * 64 + (1 - h) * 32 + 32],
                           src[:, :, g * 64 + h * 32:g * 64 + h * 32 + 32])

            def rope_proj(src, dst_fn):
                for tb in range(NTB):
                    ts = slice(tb * NB, (tb + 1) * NB)
                    pa, pb = PS(0 + 2 * (tb % 2)), PS(1 + 2 * (tb % 2))
                    for kc in range(KD):
                        MM(pa, WA[:, kc, :], src[:, kc, ts], kc == 0, kc == KD - 1)
                    for kc in range(KD):
                        MM(pb, WB[:, kc, :], src[:, kc, ts], kc == 0, kc == KD - 1)
                    TT("dve", T1[:, :], pa, ROPE[:, 0, ts], ALU.mult)
                    TT("dve", T2[:, :], pb, ROPE[:, 1, ts], ALU.mult)
                    TT("dve", dst_fn(tb), T1[:, :], T2[:, :], ALU.add)

            wq = w_q[0].rearrange("(kc p) n -> p kc n", p=128)
            wkv = w_kv.rearrange("(kc p) n -> p kc n", p=128)
            for h in range(8):
                load_w(WA, wq[:, :, h * 128:(h + 1) * 128], "wa")
                swap_halves(WB, WA)
                rope_proj(XN, lambda tb, h=h: QT[:, h, tb * NB:(tb + 1) * NB])

            T3 = Buf(aS, hS, 10240, F32, (NB,))
            T4 = Buf(aS, hS, 12288, F32, (NB,))
            T4B = Buf(aS, hS, 12288, BF16, (NB,))
            PT = [[Buf(aS, hS, 14336 + (2 * mp + b) * 1024, BF16, (NB,)) for b in range(2)] for mp in range(2)]

            for h in range(8):
                load_w(WA, wkv[:, :, h * 256:h * 256 + 128], "wa")
                swap_halves(WB, WA)
                load_w(WVb, wkv[:, :, h * 256 + 128:h * 256 + 256], "wv")
                rope_proj(XC, lambda tb: KT[:, tb * NB:(tb + 1) * NB])
                for t4 in range(4):
                    pv = PS(4 + t4 % 2)
                    for tt in range(4):
                        tok = slice((t4 * 4 + tt) * 128, (t4 * 4 + tt + 1) * 128)
                        for kc in range(KD):
                            MM(PS(4 + t4 % 2, tt * 128, (tt + 1) * 128), XC[:, kc, tok], WVb[:, kc, :], kc == 0, kc == KD - 1)
                    ACT(VV[:, t4 * NB:(t4 + 1) * NB], pv, AF.Copy)
                steps = [(qb, kt) for qb in range(NTB) for kt in range(4 * qb + 4)]

                def geom(qb, kt):
                    j = kt - 4 * qb
                    c0 = 128 * j if j > 0 else 0
                    return j, c0

                def s_stage(i):
                    qb, kt = steps[i]
                    j, c0 = geom(qb, kt)
                    qs0 = qb * NB
                    ks = slice(kt * 128, (kt + 1) * 128)
                    for mp in range(2):
                        ps_s = PS(2 * (i % 2) + mp, c0, NB)
                        p0, p1 = 64 * mp, 64 * mp + 64
                        MM(ps_s, KT[p0:p1, ks], QT[p0:p1, h, qs0 + c0:qs0 + NB], True, True)
                    for mp in range(2):
                        ps_s = PS(2 * (i % 2) + mp, c0, NB)
                        pt = PT[mp][i % 2]
                        ACT(pt[:, c0:NB], ps_s, AF.Exp, scale=0.125)
                        if j >= 0:
                            TT("pool", pt[:, c0:c0 + 128], pt[:, c0:c0 + 128], TRI, ALU.mult)

                def pv_stage(i):
                    qb, kt = steps[i]
                    j, c0 = geom(qb, kt)
                    ks = slice(kt * 128, (kt + 1) * 128)
                    first, last = kt == 0, kt == 4 * qb + 3
                    for mp in range(2):
                        pt = PT[mp][i % 2]
                        MM(PS(4 + mp, c0, NB), VV[:, ks], pt[:, c0:NB], first, last)
                        MM(PS(6 + mp, c0, NB), ONES, pt[:, c0:NB], first, last)

                def epilogue(qb):
                    qs0 = qb * NB
                    po1, po2, pl1, pl2 = PS(4), PS(5), PS(6), PS(7)
                    r1, r2, t1, t2 = T3[:, :], T4[:, :], T1[:, :], T2[:, :]
                    ACT(r1, pl1, AF.Ln)
                    ACT(r2, pl2, AF.Ln)
                    ACT(r1, r1, AF.Exp, scale=-1.0)
                    ACT(r2, r2, AF.Exp, scale=-1.0)
                    TT("dve", t1, po1, r1, ALU.mult)
                    STT("dve", t2, po2, dv("nlam"), r2, ALU.mult, ALU.mult)
                    TT("dve", t1, t1, t2, ALU.add)
                    sq = T4B[:, :]
                    TT("dve", sq, t1, t1, ALU.mult)
                    pss = PS(0)
                    MM(pss, ONESV, sq, True, True)
                    ACT(r1, pss, AF.Ln, bias=EPS)
                    ACT(r1, r1, AF.Exp, scale=-0.5)
                    STT("dve", XN[:, h, qs0:qs0 + NB], t1, dv("gsub"), r1, ALU.mult, ALU.mult)

                s_stage(0)
                for i in range(len(steps)):
                    if i + 1 < len(steps):
                        s_stage(i + 1)
                    pv_stage(i)
                    qb, kt = steps[i]
                    if kt == 4 * qb + 3:
                        epilogue(qb)

            WO = Buf(aC, hC, 0, BF16, (KD, D))
            r = WO[:, :, :]
            DMA("pool", r, w_o[0].rearrange("(kc p) n -> p kc n", p=128), "wo", writes=[r])
            for tb in range(NTB):
                for mo in range(KD):
                    ts = slice(tb * NB, (tb + 1) * NB)
                    po = PS(4 + cnt["po"] % 2)
                    cnt["po"] += 1
                    for hh in range(8):
                        MM(po, WO[:, hh, mo * 128:(mo + 1) * 128], XN[:, hh, ts], hh == 0, hh == 7)
                    TT("dve", HT[:, mo, ts], po, HT[:, mo, ts], ALU.add)
                if tail is not None:
                    tail.after_tb(tb)
            if tail is not None:
                tail.finish()

        dbg_ops = []

        def snapshot(i):
            if dbg_d is None:
                return
            v = dbg_d[i].rearrange("(c p) t -> p c t", p=128)
            dbg_ops.append(DMA("sp", v, HT[:, :, :], "dbg", reads=[HT[:, :, :]]))

        def norm_spec(st):
            if st == "ffn1_0":
                return [("ffn1_norm0", dst_of(XN))]
            if st == "rg":
                return [("mix_norm0", dst_of(XN))]
            if st == "ffn2_0":
                return [("ffn2_norm0", dst_of(XN))]
            if st == "ffn1_1":
                return [("ffn1_norm1", dst_of(XN)), ("kv_norm", dst_of(XC))]
            if st == "attn":
                g = [("mix_norm1", dst_of(XN))]
                if "ffn1_1" not in stages:
                    g.append(("kv_norm", dst_of(XC)))
                return g
            if st == "ffn2_1":
                return [("ffn2_norm1", dst_of(XN))]
            raise ValueError(st)

        OUTS = [Buf(aC, hC, i * 16384, F32, (KD, NB)) for i in range(2)]
        outT_v = outT.rearrange("(c p) t -> p c t", p=128)
        out_ops = []

        def out_post(tb):
            ts = slice(tb * NB, (tb + 1) * NB)
            ob = OUTS[tb % 2]
            out_ops.append(DMA("sp", outT_v[:, :, ts], ob[:, :, :], "out%d" % (tb % 2), reads=[ob[:, :, :]]))

        final_tail = NormTail([("final_norm", lambda c, tb: OUTS[tb % 2][:, c, :])], post=out_post)

        emit_norm(norm_spec(stages[0]))
        for si, st in enumerate(stages):
            last = si == len(stages) - 1
            use_tail = True
            if last:
                tail = final_tail if use_tail else None
            else:
                tail = NormTail(norm_spec(stages[si + 1])) if use_tail else None
            if st in ("ffn1_0", "ffn1_1"):
                emit_ffn("ffn1", int(st[-1]), tail)
            elif st in ("ffn2_0", "ffn2_1"):
                emit_ffn("ffn2", int(st[-1]), tail)
            elif st == "rg":
                emit_rg(tail)
            elif st == "attn":
                emit_attn(tail)
            snapshot(si)
            if not use_tail:
                if last:
                    for tb in range(NTB):
                        norm_n1(tb)
                        final_tail._n2(tb)
                else:
                    emit_norm(norm_spec(stages[si + 1]))

        fence = P.add("sp", None)
        fence.deps = list(out_ops) + list(dbg_ops)

        P.emit(nc)
    return nc


def _host_consts():
    pos = np.arange(T, dtype=np.float32)
    inv_freq = (10000.0 ** (-np.arange(0, 64, 2, dtype=np.float32) / 64.0)).astype(np.float32)
    ang = pos[None, :] * inv_freq[:, None]
    cos = np.cos(ang).astype(np.float32)
    sin = np.sin(ang).astype(np.float32)
    cos_t = np.concatenate([cos, cos, cos, cos], axis=0)
    sin_t = np.concatenate([-sin, sin, -sin, sin], axis=0)
    rope = np.concatenate([cos_t, sin_t], axis=1).astype(np.float32)
    c16 = np.zeros((128, 4, 128), dtype=np.float32)
    c16[:, 0, :] = 1.0
    c16[:, 1, :] = 1.0 / 1024.0
    c16[:, 2, :] = 1.0 / 128.0
    kk = np.arange(128)[:, None]
    qq = np.arange(128)[None, :]
    c16[:, 3, :] = (qq >= kk).astype(np.float32)
    return rope, c16.reshape(128, 512).astype(ml_dtypes.bfloat16)


def _pack_cvec(inp):
    cv = np.zeros((128, NCV), dtype=np.float32)

    def put(name, vec):
        vec = np.asarray(vec, dtype=np.float32).reshape(-1)
        n = vec.shape[0] // 128
        cv[:, CV[name]:CV[name] + n] = vec.reshape(n, 128).T

    put("ffn1_norm0", inp["ffn1_norm"][0]); put("ffn1_norm1", inp["ffn1_norm"][1])
    put("mix_norm0", inp["mix_norm"][0]); put("mix_norm1", inp["mix_norm"][1])
    put("ffn2_norm0", inp["ffn2_norm"][0]); put("ffn2_norm1", inp["ffn2_norm"][1])
    put("kv_norm", inp["kv_norm"]); put("final_norm", inp["final_norm"])
    put("rg_b_in", inp["rg_b_in"][0])
    put("rg_conv_w", inp["rg_conv_w"][0])
    put("rg_conv_b", inp["rg_conv_b"][0])
    put("rg_gate_b", inp["rg_gate_b"][0])
    put("rg_lambda", inp["rg_lambda"][0])
    put("rg_b_out", inp["rg_b_out"][0])
    put("subln", inp["diff_subln"][0])
    cv[:, CV["dlam"]:CV["dlam"] + 256] = np.asarray(inp["diff_lambda"][0], dtype=np.float32).reshape(1, 256)
    return cv


_CACHE = {}


def run(inputs, stages=ALL_STAGES, dbg=False, cores=8, trace=False):
    key = (tuple(stages), dbg)
    if key not in _CACHE:
        _CACHE[key] = build_program(stages, dbg)
    nc = _CACHE[key]
    rope, c16 = _host_consts()
    cvec = _pack_cvec(inputs)
    shared = {
        "cvec": cvec, "rope": rope, "c16": c16,
        "ffn1_w_in": np.ascontiguousarray(inputs["ffn1_w_in"], dtype=np.float32),
        "ffn1_w_out": np.ascontiguousarray(inputs["ffn1_w_out"], dtype=np.float32),
        "ffn2_w_in": np.ascontiguousarray(inputs["ffn2_w_in"], dtype=np.float32),
        "ffn2_w_out": np.ascontiguousarray(inputs["ffn2_w_out"], dtype=np.float32),
        "rg_w_in": np.ascontiguousarray(inputs["rg_w_in"], dtype=np.float32),
        "rg_gate_w": np.ascontiguousarray(inputs["rg_gate_w"], dtype=np.float32),
        "rg_w_out": np.ascontiguousarray(inputs["rg_w_out"], dtype=np.float32),
        "w_kv": np.ascontiguousarray(inputs["w_kv"], dtype=np.float32),
        "diff_w_q": np.ascontiguousarray(inputs["diff_w_q"], dtype=np.float32),
        "diff_w_o": np.ascontiguousarray(inputs["diff_w_o"], dtype=np.float32),
    }
    x = np.asarray(inputs["x"], dtype=np.float32)
    in_maps = []
    for b in range(cores):
        m = dict(shared)
        m["xT"] = np.ascontiguousarray(x[b].T)
        in_maps.append(m)
    res = run_bass_kernel_spmd(nc, in_maps, core_ids=list(range(cores)), trace=trace)
    return res


def kernel(**inputs):
    res = run(inputs)
    out = np.stack([np.asarray(r["outT"], dtype=np.float32).T for r in res.results], axis=0)
    return np.ascontiguousarray(out)
```

```python
import math
from contextlib import ExitStack

import numpy as np
import ml_dtypes
import concourse.bass as bass
import concourse.mybir as mybir
from concourse.bass_utils import run_bass_kernel_spmd

F32 = mybir.dt.float32
BF16 = mybir.dt.bfloat16
AF = mybir.ActivationFunctionType
ALU = mybir.AluOpType

T = 2048
D = 1024
KD = 8
FF = 2816
NFF = 22
NB = 512
NTB = 4
EPS = 1e-6
GRAN = 64
LAMBDA_INIT = 0.8 - 0.6 * math.exp(-0.3 * 1)

ALL_STAGES = ("ffn1_0", "rg", "ffn2_0", "ffn1_1", "attn", "ffn2_1")

CV = {}
_c = 0
for _name, _n in (("ffn1_norm0", 8), ("ffn1_norm1", 8), ("mix_norm0", 8), ("mix_norm1", 8),
                  ("ffn2_norm0", 8), ("ffn2_norm1", 8), ("kv_norm", 8), ("final_norm", 8),
                  ("rg_b_in", 16), ("rg_conv_w", 32), ("rg_conv_b", 8), ("rg_gate_b", 16),
                  ("rg_lambda", 8), ("rg_b_out", 8), ("subln", 1), ("dlam", 256)):
    CV[_name] = _c
    _c += _n
NCV = _c
DV = {}
_c = 0
for _name, _n in (("c8", 8), ("c16", 8), ("ngb", 16), ("lam_s", 2), ("lam_e", 2), ("nlam", 1),
                  ("gsub", 1), ("tmp", 8), ("hist", 24), ("hst", 8), ("lp", 128)):
    DV[_name] = _c
    _c += _n
NDV = _c


class Arena:
    def __init__(self, name, nbytes):
        n = (nbytes + GRAN - 1) // GRAN + 1
        self.name = name
        self.last_w = [None] * n
        self.readers = [None] * n


class Ref:
    __slots__ = ("ap", "rngs")

    def __init__(self, ap, rngs):
        self.ap = ap
        self.rngs = rngs


class Op:
    __slots__ = ("eng", "fn", "deps", "inc", "semval", "dma_sem", "dma_val", "seq")


class Buf:
    def __init__(self, arena, ap_f32, byte_off, dt, dims):
        self.arena = arena
        self.off = byte_off
        self.dt = dt
        self.dims = tuple(dims)
        self.es = 4 if dt == F32 else 2
        n = int(np.prod(dims))
        assert byte_off % 4 == 0 and (n * self.es) % 4 == 0
        ap = ap_f32[:, byte_off // 4:(byte_off + n * self.es) // 4]
        if dt != F32:
            ap = ap.bitcast(dt)
        if len(dims) == 2:
            ap = ap.rearrange("p (a b) -> p a b", a=dims[0])
        elif len(dims) == 3:
            ap = ap.rearrange("p (a b c) -> p a b c", a=dims[0], b=dims[1])
        elif len(dims) == 4:
            ap = ap.rearrange("p (a b c d) -> p a b c d", a=dims[0], b=dims[1], c=dims[2])
        self.full = ap
        self.nbytes = n * self.es

    def __getitem__(self, ix):
        if not isinstance(ix, tuple):
            ix = (ix,)
        ap = self.full[ix]
        fix = list(ix[1:]) + [slice(None)] * (len(self.dims) - len(ix) + 1)
        bounds = []
        for d, i in zip(self.dims, fix):
            if isinstance(i, int):
                bounds.append((i, i + 1))
            else:
                assert i.step is None
                bounds.append((0 if i.start is None else i.start, d if i.stop is None else i.stop))
        k = len(self.dims)
        strides = [1] * k
        for i in range(k - 2, -1, -1):
            strides[i] = strides[i + 1] * self.dims[i + 1]
        j = k - 1
        while j > 0 and bounds[j] == (0, self.dims[j]):
            j -= 1
        run_len = (bounds[j][1] - bounds[j][0]) * strides[j]
        outs = [bounds[j][0] * strides[j]]
        for i in range(j - 1, -1, -1):
            outs = [o + t * strides[i] for t in range(*bounds[i]) for o in outs]
        if len(outs) > 64:
            lo = min(outs)
            hi = max(outs) + run_len
            rngs = [(self.arena, self.off + lo * self.es, self.off + hi * self.es)]
        else:
            rngs = [(self.arena, self.off + o * self.es, self.off + (o + run_len) * self.es) for o in outs]
        return Ref(ap, rngs)


class Prog:
    ENGS = ("pe", "act", "dve", "pool", "sp")

    def __init__(self):
        self.ops = {e: [] for e in self.ENGS}
        self.dma_count = {}
        self.seq = 0

    def add(self, eng, fn, reads=(), writes=(), dma=None):
        op = Op()
        op.eng = eng
        op.fn = fn
        op.inc = False
        op.semval = None
        op.dma_sem = dma
        op.dma_val = None
        self.seq += 1
        op.seq = self.seq
        deps = {}

        def add_dep(o):
            if o is None:
                return
            if o.dma_sem is not None:
                key = ("dma", id(o))
            else:
                if o.eng == eng and eng == "pe":
                    return
                key = o.eng
            cur = deps.get(key)
            if cur is None or o.seq > cur.seq:
                deps[key] = o

        for r in reads:
            for (ar, lo, hi) in r.rngs:
                for g in range(lo // GRAN, (hi - 1) // GRAN + 1):
                    add_dep(ar.last_w[g])
        for w in writes:
            for (ar, lo, hi) in w.rngs:
                for g in range(lo // GRAN, (hi - 1) // GRAN + 1):
                    add_dep(ar.last_w[g])
                    rd = ar.readers[g]
                    if rd:
                        for o in rd.values():
                            add_dep(o)
        rkey = eng if dma is None else ("dma", id(op))
        for r in reads:
            for (ar, lo, hi) in r.rngs:
                for g in range(lo // GRAN, (hi - 1) // GRAN + 1):
                    rd = ar.readers[g]
                    if rd is None:
                        rd = ar.readers[g] = {}
                    rd[rkey] = op
        for w in writes:
            for (ar, lo, hi) in w.rngs:
                for g in range(lo // GRAN, (hi - 1) // GRAN + 1):
                    ar.last_w[g] = op
                    ar.readers[g] = None
        op.deps = list(deps.values())
        if dma is not None:
            c = self.dma_count.get(dma, 0) + 1
            self.dma_count[dma] = c
            op.dma_val = 16 * c
        self.ops[eng].append(op)
        return op

    def emit(self, nc):
        for e in self.ENGS:
            for op in self.ops[e]:
                for d in op.deps:
                    if d.dma_sem is None:
                        d.inc = True
        for e in self.ENGS:
            c = 0
            for op in self.ops[e]:
                if op.inc and op.dma_sem is None:
                    c += 1
                    op.semval = c
        with ExitStack() as es:
            sems = {}
            for e in ("pe", "act", "dve", "pool"):
                sems[e] = es.enter_context(nc.semaphore("s_" + e))
            for name in self.dma_count:
                sems[name] = es.enter_context(nc.semaphore("d_" + name))
            block = es.enter_context(nc.Block())

            def run(e, engobj):
                known = {}
                for op in self.ops[e]:
                    need = {}
                    for d in op.deps:
                        if d.dma_sem is not None:
                            key, v = d.dma_sem, d.dma_val
                        else:
                            key, v = d.eng, d.semval
                        if known.get(key, 0) >= v:
                            continue
                        if need.get(key, 0) < v:
                            need[key] = v
                    for key, v in need.items():
                        engobj.wait_ge(sems[key], v)
                        known[key] = v
                    if op.fn is not None:
                        ins = op.fn(engobj)
                        if op.dma_sem is not None:
                            ins.then_inc(sems[op.dma_sem], 16)
                        elif op.inc:
                            ins.then_inc(sems[e], 1)

            @block.tensor
            def _(eng):
                run("pe", eng)

            @block.scalar
            def _(eng):
                run("act", eng)

            @block.vector
            def _(eng):
                run("dve", eng)

            @block.gpsimd
            def _(eng):
                run("pool", eng)

            @block.sync
            def _(eng):
                run("sp", eng)


def build_program(stages=ALL_STAGES, dbg=False):
    nc = bass.Bass("TRN2", target_bir_lowering=False)
    P = Prog()

    def dram_in(name, shape, dt=F32):
        return nc.dram_tensor(name, list(shape), dt, kind="ExternalInput").ap()

    xT = dram_in("xT", [D, T])
    cvec_d = dram_in("cvec", [128, NCV])
    rope_d = dram_in("rope", [128, 2 * T])
    c16_d = dram_in("c16", [128, 512], BF16)
    ffn_w_in = {"ffn1": dram_in("ffn1_w_in", [2, D, 2 * FF]), "ffn2": dram_in("ffn2_w_in", [2, D, 2 * FF])}
    ffn_w_out = {"ffn1": dram_in("ffn1_w_out", [2, FF, D]), "ffn2": dram_in("ffn2_w_out", [2, FF, D])}
    rg_w_in = dram_in("rg_w_in", [1, D, 2 * D])
    rg_gate_w = dram_in("rg_gate_w", [1, 4, 256, 512])
    rg_w_out = dram_in("rg_w_out", [1, D, D])
    w_kv = dram_in("w_kv", [D, 2048])
    w_q = dram_in("diff_w_q", [1, D, D])
    w_o = dram_in("diff_w_o", [1, D, D])
    outT = nc.dram_tensor("outT", [D, T], F32, kind="ExternalOutput").ap()
    dbg_d = None
    if dbg:
        dbg_d = nc.dram_tensor("dbg", [len(stages), D, T], F32, kind="ExternalOutput").ap()

    with ExitStack() as es:
        es.enter_context(nc.allow_low_precision("bf16 matmul operands, fp32 accumulation"))

        def sb(name, nbytes):
            h = es.enter_context(nc.sbuf_tensor(name, [128, nbytes // 4], F32))
            return Arena(name, nbytes), h[:]

        aH, hH = sb("HT", KD * T * 4)
        aN, hN = sb("XN", KD * T * 2)
        aC, hC = sb("XC", KD * T * 2)
        DBYTES = 56 * 1024
        aD, hD = sb("DD", DBYTES)
        EC_BYTES = (NCV + NDV) * 4 + 1024
        aE, hE = sb("EC", EC_BYTES)
        SBYTES = 20 * 1024
        aS, hS = sb("SS", SBYTES)

        HT = Buf(aH, hH, 0, F32, (KD, T))
        XN = Buf(aN, hN, 0, BF16, (KD, T))
        XC = Buf(aC, hC, 0, BF16, (KD, T))
        CVB = Buf(aE, hE, 0, F32, (NCV,))
        DVB = Buf(aE, hE, NCV * 4, F32, (NDV,))
        C16 = Buf(aE, hE, (NCV + NDV) * 4, BF16, (4, 128))

        psums = []
        for i in range(8):
            h = es.enter_context(nc.psum_tensor("ps%d" % i, [128, NB], F32))
            ar = Arena("ps%d" % i, NB * 4)
            psums.append((ar, h[:]))

        def PS(i, c0=0, c1=NB, p0=0, p1=128):
            ar, ap = psums[i]
            return Ref(ap[p0:p1, c0:c1], [(ar, c0 * 4, c1 * 4)])

        def cv(name, c=0, n=1):
            o = CV[name] + c
            return CVB[:, o:o + n]

        def dv(name, c=0, n=1):
            o = DV[name] + c
            return DVB[:, o:o + n]

        ONES = C16[:, 0, :]
        ONESM = C16[:, 1, :]
        ONESV = C16[:, 2, :]
        TRI = C16[:, 3, :]

        def MM(out, lhsT, rhs, start, stop):
            P.add("pe", lambda e: e.matmul(out.ap, lhsT.ap, rhs.ap, start=start, stop=stop),
                  reads=[lhsT, rhs], writes=[out])

        def _a(x):
            return x.ap if isinstance(x, Ref) else x

        def ACT(out, in_, func, bias=0.0, scale=1.0):
            rd = [in_] + [x for x in (bias, scale) if isinstance(x, Ref)]
            P.add("act", lambda e: e.activation(out.ap, in_.ap, func, bias=_a(bias), scale=_a(scale)),
                  reads=rd, writes=[out])

        def TS(eng, out, in0, s1, s2, op0, op1=None):
            rd = [in0] + [x for x in (s1, s2) if isinstance(x, Ref)]
            if op1 is None:
                P.add(eng, lambda e: e.tensor_scalar(out.ap, in0.ap, _a(s1), None, op0), reads=rd, writes=[out])
            else:
                P.add(eng, lambda e: e.tensor_scalar(out.ap, in0.ap, _a(s1), _a(s2), op0, op1), reads=rd, writes=[out])

        def STT(eng, out, in0, s, in1, op0, op1):
            rd = [in0, in1] + ([s] if isinstance(s, Ref) else [])
            P.add(eng, lambda e: e.scalar_tensor_tensor(out.ap, in0.ap, _a(s), in1.ap, op0, op1), reads=rd, writes=[out])

        def TT(eng, out, in0, in1, op):
            P.add(eng, lambda e: e.tensor_tensor(out.ap, in0.ap, in1.ap, op), reads=[in0, in1], writes=[out])

        def SIGFIN(x):
            TS("dve", x, x, 1.0, None, ALU.add)
            P.add("dve", lambda e: e.reciprocal(x.ap, x.ap), reads=[x], writes=[x])

        def CP(eng, out, in_):
            P.add(eng, lambda e: e.tensor_copy(out.ap, in_.ap), reads=[in_], writes=[out])

        def DMA(eng, out, in_, sem, reads=(), writes=()):
            oa = out.ap if isinstance(out, Ref) else out
            ia = in_.ap if isinstance(in_, Ref) else in_
            return P.add(eng, lambda e: e.dma_start(out=oa, in_=ia), reads=reads, writes=writes, dma=sem)

        DMA("sp", CVB[:, :], cvec_d, "cvec", writes=[CVB[:, :]])
        DMA("sp", C16[:, :, :], c16_d.rearrange("p (a b) -> p a b", a=4), "c16", writes=[C16[:, :, :]])
        xT_v = xT.rearrange("(c p) t -> p c t", p=128)
        for tb in range(NTB):
            r = HT[:, :, tb * NB:(tb + 1) * NB]
            DMA("sp", r, xT_v[:, :, tb * NB:(tb + 1) * NB], "x%d" % tb, writes=[r])

        SQ = [Buf(aS, hS, i * 1024, BF16, (NB,)) for i in range(4)]
        RSTD = [Buf(aS, hS, 4096 + i * 2048, F32, (NB,)) for i in range(2)]
        SG = [Buf(aS, hS, 8192 + i * 2048, F32, (NB,)) for i in range(2)]
        cnt = {"sq": 0, "rstd": 0, "sg": 0, "pp": 0, "po": 0}

        def norm_n1(tb):
            ts = slice(tb * NB, (tb + 1) * NB)
            pn = PS(7)
            for c in range(KD):
                sq = SQ[cnt["sq"] % 4][:, :]
                cnt["sq"] += 1
                ACT(sq, HT[:, c, ts], AF.Square)
                MM(pn, ONESM, sq, c == 0, c == KD - 1)
            rstd = RSTD[tb % 2][:, :]
            ACT(rstd, pn, AF.Ln, bias=EPS)
            ACT(rstd, rstd, AF.Exp, scale=-0.5)

        def norm_n2(tb, gains_dsts):
            ts = slice(tb * NB, (tb + 1) * NB)
            rstd = RSTD[tb % 2][:, :]
            for gname, dst in gains_dsts:
                for c in range(KD):
                    STT("dve", dst(c, tb), HT[:, c, ts], cv(gname, c), rstd, ALU.mult, ALU.mult)

        def emit_norm(gains_dsts):
            for tb in range(NTB):
                norm_n1(tb)
                norm_n2(tb, gains_dsts)

        class NormTail:
            def __init__(self, gains_dsts, post=None):
                self.g = gains_dsts
                self.post = post

            def _n2(self, tb):
                norm_n2(tb, self.g)
                if self.post is not None:
                    self.post(tb)

            def after_tb(self, k):
                if k >= 2:
                    self._n2(k - 2)
                if k >= 1:
                    norm_n1(k - 1)

            def finish(self):
                self._n2(NTB - 2)
                norm_n1(NTB - 1)
                self._n2(NTB - 1)

        def dst_of(buf):
            return lambda c, tb: buf[:, c, tb * NB:(tb + 1) * NB]

        FGROUPS = [[0, 1, 2, 3, 4], [5, 6, 7, 8, 9], [10, 11, 12, 13], [14, 15, 16, 17], [18, 19, 20, 21]]
        G = Buf(aD, hD, 0, BF16, (5, T))
        WIN = [Buf(aD, hD, 20480 + i * 4096, BF16, (2, KD, 128)) for i in range(4)]
        WOUT = [Buf(aD, hD, 36864 + i * 2048, BF16, (D,)) for i in range(10)]

        def emit_ffn(which, l, tail=None):
            w_in = ffn_w_in[which][l].rearrange("(kc p) (two m n) -> p two kc m n", p=128, two=2, m=NFF)
            w_out = ffn_w_out[which][l].rearrange("(m p) n -> p m n", p=128)
            tag = which + str(l)

            def load_win(m):
                s = m % 4
                r = WIN[s][:, :, :, :]
                DMA("pool", r, w_in[:, :, :, m, :], "win%d" % s, writes=[r])

            def load_wout(m):
                s = m % 10
                r = WOUT[s][:, :]
                DMA("pool", r, w_out[:, m, :], "wout%d" % s, writes=[r])

            for m in range(3):
                load_win(m)
            for m in FGROUPS[0]:
                load_wout(m)
            for gi, grp in enumerate(FGROUPS):
                for j, m in enumerate(grp):
                    if m + 3 < NFF:
                        load_win(m + 3)
                    W = WIN[m % 4]
                    for tb in range(NTB):
                        ts = slice(tb * NB, (tb + 1) * NB)
                        pp = cnt["pp"] % 2
                        cnt["pp"] += 1
                        pg, pu = PS(2 * pp), PS(2 * pp + 1)
                        for kc in range(KD):
                            MM(pg, W[:, 0, kc, :], XN[:, kc, ts], kc == 0, kc == KD - 1)
                        for kc in range(KD):
                            MM(pu, W[:, 1, kc, :], XN[:, kc, ts], kc == 0, kc == KD - 1)
                        sg = SG[cnt["sg"] % 2][:, :]
                        cnt["sg"] += 1
                        ACT(sg, pg, AF.Silu)
                        TT("dve", G[:, j, ts], sg, pu, ALU.mult)
                if gi + 1 < len(FGROUPS):
                    for m in FGROUPS[gi + 1]:
                        load_wout(m)
                lastg = gi == len(FGROUPS) - 1
                order = ([(mo, tb) for tb in range(NTB) for mo in range(KD)] if lastg
                         else [(mo, tb) for mo in range(KD) for tb in range(NTB)])
                for (mo, tb) in order:
                    ts = slice(tb * NB, (tb + 1) * NB)
                    po = PS(4 + cnt["po"] % 2)
                    cnt["po"] += 1
                    for j, m in enumerate(grp):
                        MM(po, WOUT[m % 10][:, mo * 128:(mo + 1) * 128], G[:, j, ts], j == 0, j == len(grp) - 1)
                    STT("dve", HT[:, mo, ts], po, 0.5, HT[:, mo, ts], ALU.mult, ALU.add)
                    if lastg and tail is not None and mo == KD - 1:
                        tail.after_tb(tb)
            if tail is not None:
                tail.finish()

        def emit_rg(tail=None):
            WRI = Buf(aD, hD, 0, BF16, (KD, 2 * D))
            WRG = Buf(aD, hD, 32768, BF16, (4, 2, 512))
            WRO = Buf(aD, hD, 40960, BF16, (KD, D))
            wi = rg_w_in[0].rearrange("(kc p) n -> p kc n", p=128)
            for half in range(2):
                r = WRI[:, :, half * D:(half + 1) * D]
                DMA("pool", r, wi[:, :, half * D:(half + 1) * D], "wri%d" % half, writes=[r])
            r = WRG[:, :, :, :]
            DMA("pool", r, rg_gate_w[0].rearrange("n (k p) g -> p n k g", p=128), "wrg", writes=[r])
            r = WRO[:, :, :]
            DMA("pool", r, rg_w_out[0].rearrange("(kc p) n -> p kc n", p=128), "wro", writes=[r])

            tmp = dv("tmp", 0, 8)
            ACT(tmp, cv("rg_lambda", 0, 8), AF.Exp, scale=-1.0)
            ACT(tmp, tmp, AF.Ln, bias=1.0)
            TS("dve", dv("c8", 0, 8), tmp, -8.0, None, ALU.mult)
            TS("dve", dv("c16", 0, 8), tmp, -16.0, None, ALU.mult)
            TS("dve", dv("ngb", 0, 16), cv("rg_gate_b", 0, 16), -1.0, None, ALU.mult)
            P.add("dve", lambda e: e.memset(dv("hist", 0, 24).ap, 0.0), writes=[dv("hist", 0, 24)])
            P.add("dve", lambda e: e.memset(dv("hst", 0, 8).ap, 0.0), writes=[dv("hst", 0, 8)])

            def tb_(i, n=NB, dt=F32):
                return Buf(aC, hC, i * 2048, dt, (n,))
            HG = Buf(aC, hC, 0, BF16, (KD, NB))
            XB = Buf(aC, hC, 4 * 2048, F32, (NB + 4,))
            XCF = [[tb_(6), tb_(7)], [Buf(aS, hS, 14336, F32, (NB,)), Buf(aS, hS, 18432, F32, (NB,))]]
            XCB = [[Buf(aC, hC, 8 * 2048, BF16, (NB,)), Buf(aC, hC, 8 * 2048 + 1024, BF16, (NB,))],
                   [Buf(aC, hC, 13 * 2048, BF16, (NB,)), Buf(aC, hC, 13 * 2048 + 1024, BF16, (NB,))]]
            TSETS = [[tb_(9), tb_(10), tb_(11), tb_(12), None, tb_(14)],
                     [tb_(15), Buf(aS, hS, 8192, F32, (NB,)), Buf(aS, hS, 10240, F32, (NB,)),
                      Buf(aS, hS, 12288, F32, (NB,)), None, Buf(aS, hS, 16384, F32, (NB,))]]
            KG = math.sqrt(2.0 / math.pi)
            items = [(tb, n) for tb in range(NTB) for n in range(4)]

            def stage_a(i):
                tb, n = items[i]
                ts = slice(tb * NB, (tb + 1) * NB)
                sl = i % 2
                for jj in range(2):
                    c = 2 * n + jj
                    px = PS(0)
                    for kc in range(KD):
                        MM(px, WRI[:, kc, D + c * 128:D + (c + 1) * 128], XN[:, kc, ts], kc == 0, kc == KD - 1)
                    CP("pool", XB[:, 0:3], dv("hist", 3 * c, 3))
                    TS("dve", XB[:, 3:3 + NB], px, cv("rg_b_in", 8 + c), None, ALU.add)
                    xc = XCF[sl][jj][:, :]
                    TS("dve", xc, XB[:, 0:NB], cv("rg_conv_w", 0 * 8 + c), cv("rg_conv_b", c), ALU.mult, ALU.add)
                    for k in range(1, 4):
                        STT("dve", xc, XB[:, k:k + NB], cv("rg_conv_w", k * 8 + c), xc, ALU.mult, ALU.add)
                    CP("pool", dv("hist", 3 * c, 3), XB[:, NB:NB + 3])

            def stage_cast(i):
                sl = i % 2
                for jj in range(2):
                    ACT(XCB[sl][jj][:, :], XCF[sl][jj][:, :], AF.Copy)

            def stage_b(i, part):
                tb, n = items[i]
                ts = slice(tb * NB, (tb + 1) * NB)
                sl = i % 2

                def ctx(jj):
                    return 2 * n + jj, XCF[sl][jj][:, :], TSETS[jj]

                def st_gb_mm(jj):
                    c, xc, (B4, B5, B6, B7, B8, B9) = ctx(jj)
                    pgb = PS(3 if jj == 0 else 5)
                    for kc in range(KD):
                        MM(pgb, WRI[:, kc, c * 128:(c + 1) * 128], XN[:, kc, ts], kc == 0, kc == KD - 1)
                    ACT(B9[:, :], pgb, AF.Gelu_apprx_tanh, bias=cv("rg_b_in", c))

                def st_gates_mm(jj):
                    c, xc, (B4, B5, B6, B7, B8, B9) = ctx(jj)
                    pgx, pga = (PS(1), PS(2)) if jj == 0 else (PS(6), PS(7))
                    for k in range(2):
                        MM(pgx, WRG[:, n, k, jj * 128:(jj + 1) * 128], XCB[sl][k][:, :], k == 0, k == 1)
                    for k in range(2):
                        MM(pga, WRG[:, n, k, 256 + jj * 128:256 + (jj + 1) * 128], XCB[sl][k][:, :], k == 0, k == 1)
                    ACT(B4[:, :], pgx, AF.Exp, bias=dv("ngb", n * 4 + jj), scale=-1.0)
                    ACT(B5[:, :], pga, AF.Exp, bias=dv("ngb", n * 4 + 2 + jj), scale=-1.0)

                def st_v(jj):
                    c, xc, (B4, B5, B6, B7, B8, B9) = ctx(jj)
                    TT("pool", B9[:, :], B8[:, :], B8[:, :], ALU.mult)
                    TS("pool", B9[:, :], B9[:, :], 0.044715, 1.0, ALU.mult, ALU.add)
                    TT("pool", B9[:, :], B9[:, :], B8[:, :], ALU.mult)

                def st_r1(jj):
                    c, xc, (B4, B5, B6, B7, B8, B9) = ctx(jj)
                    ACT(B5[:, :], B5[:, :], AF.Ln, bias=1.0)

                def st_r2(jj):
                    c, xc, (B4, B5, B6, B7, B8, B9) = ctx(jj)
                    ACT(B5[:, :], B5[:, :], AF.Exp, scale=-1.0)

                def st_a(jj):
                    c, xc, (B4, B5, B6, B7, B8, B9) = ctx(jj)
                    ACT(B6[:, :], B5[:, :], AF.Exp, scale=dv("c8", c))
                    ACT(B7[:, :], B5[:, :], AF.Exp, scale=dv("c16", c))

                def st_ln(jj):
                    c, xc, (B4, B5, B6, B7, B8, B9) = ctx(jj)
                    ACT(B4[:, :], B4[:, :], AF.Ln, bias=1.0)
                    ACT(B7[:, :], B7[:, :], AF.Ln, bias=1.0, scale=-1.0)
                    STT("dve", B7[:, :], B7[:, :], 0.5, B4[:, :], ALU.mult, ALU.subtract)

                def st_m(jj):
                    c, xc, (B4, B5, B6, B7, B8, B9) = ctx(jj)
                    ACT(B7[:, :], B7[:, :], AF.Exp)
                    TT("dve", B4[:, :], B7[:, :], xc, ALU.mult)

                def st_scan(jj):
                    c, xc, (B4, B5, B6, B7, B8, B9) = ctx(jj)
                    hs = B5[:, :]
                    P.add("dve", lambda e, hs=hs, c=c, B6=B6, B4=B4: e.tensor_tensor_scan(
                        hs.ap, B6[:, :].ap, B4[:, :].ap, dv("hst", c).ap, ALU.mult, ALU.add),
                        reads=[B6[:, :], B4[:, :], dv("hst", c)], writes=[hs])
                    CP("pool", dv("hst", c), B5[:, NB - 1:NB])

                def st_g1(jj):
                    c, xc, (B4, B5, B6, B7, B8, B9) = ctx(jj)
                    ACT(B9[:, :], B9[:, :], AF.Exp, scale=-2.0 * KG)

                def st_g2(jj):
                    c, xc, (B4, B5, B6, B7, B8, B9) = ctx(jj)
                    ACT(B9[:, :], B9[:, :], AF.Ln, bias=1.0)

                def st_g3(jj):
                    c, xc, (B4, B5, B6, B7, B8, B9) = ctx(jj)
                    ACT(B9[:, :], B9[:, :], AF.Exp, scale=-1.0)

                def st_hg(jj):
                    c, xc, (B4, B5, B6, B7, B8, B9) = ctx(jj)
                    TT("pool", HG[:, c, :], B5[:, :], B9[:, :], ALU.mult)

                parts = {"b1a": (st_gb_mm, st_gates_mm, st_r1, st_r2, st_a),
                         "b1b": (st_ln,),
                         "b2": (st_m, st_scan, st_hg)}
                for st in parts[part]:
                    for jj in range(2):
                        st(jj)

            def outproj(tb):
                ts = slice(tb * NB, (tb + 1) * NB)
                for mo in range(KD):
                    po = PS(4 + cnt["po"] % 2)
                    cnt["po"] += 1
                    for kc in range(KD):
                        MM(po, WRO[:, kc, mo * 128:(mo + 1) * 128], HG[:, kc, :], kc == 0, kc == KD - 1)
                    STT("dve", HT[:, mo, ts], po, cv("rg_b_out", mo), HT[:, mo, ts], ALU.add, ALU.add)

            stage_a(0)
            stage_cast(0)
            pending = None
            for i in range(len(items)):
                stage_b(i, "b1a")
                if i + 1 < len(items):
                    stage_a(i + 1)
                if pending is not None:
                    outproj(pending)
                    if tail is not None:
                        tail.after_tb(pending)
                    pending = None
                stage_b(i, "b1b")
                stage_b(i, "b2")
                if i + 1 < len(items):
                    stage_cast(i + 1)
                if items[i][1] == 3:
                    pending = items[i][0]
            outproj(pending)
            if tail is not None:
                tail.after_tb(pending)
                tail.finish()

        def emit_attn(tail=None):
            QT = Buf(aD, hD, 0, BF16, (KD, T))
            ROPE = Buf(aD, hD, 32768, F32, (2, T))
            KT = Buf(aD, hD, 49152, BF16, (T,))
            VV = Buf(aD, hD, 53248, BF16, (T,))
            WA = Buf(aS, hS, 0, BF16, (KD, 128))
            WB = Buf(aS, hS, 2048, BF16, (KD, 128))
            WVb = Buf(aS, hS, 4096, BF16, (KD, 128))
            T1 = Buf(aS, hS, 6144, F32, (NB,))
            T2 = Buf(aS, hS, 8192, F32, (NB,))

            r = ROPE[:, :, :]
            DMA("sp", r, rope_d.rearrange("p (a t) -> p a t", a=2), "rope", writes=[r])

            TT("dve", dv("lp", 0, 64), cv("dlam", 0, 64), cv("dlam", 64, 64), ALU.mult)
            TT("dve", dv("lp", 64, 64), cv("dlam", 128, 64), cv("dlam", 192, 64), ALU.mult)
            for i in range(2):
                P.add("dve", lambda e, i=i: e.tensor_reduce(dv("lam_s", i).ap, dv("lp", 64 * i, 64).ap,
                                                            mybir.AxisListType.X, ALU.add),
                      reads=[dv("lp", 64 * i, 64)], writes=[dv("lam_s", i)])
            ACT(dv("lam_e", 0, 2), dv("lam_s", 0, 2), AF.Exp)
            TT("dve", dv("nlam"), dv("lam_e", 1), dv("lam_e", 0), ALU.subtract)
            TS("dve", dv("nlam"), dv("nlam"), -LAMBDA_INIT, None, ALU.add)
            TS("dve", dv("gsub"), cv("subln"), 1.0 - LAMBDA_INIT, None, ALU.mult)

            def load_w(dst, src_ap, sem):
                r = dst[:, :, :]
                DMA("pool", r, src_ap, sem, writes=[r])

            def swap_halves(dst, src):
                for g in range(2):
                    for h in range(2):
                        CP("pool", dst[:, :, g * 64 + (1 - h) * 32:g * 64 + (1 - h) * 32 + 32],
                           src[:, :, g * 64 + h * 32:g * 64 + h * 32 + 32])

            def rope_proj(src, dst_fn):
                for tb in range(NTB):
                    ts = slice(tb * NB, (tb + 1) * NB)
                    pa, pb = PS(0 + 2 * (tb % 2)), PS(1 + 2 * (tb % 2))
                    for kc in range(KD):
                        MM(pa, WA[:, kc, :], src[:, kc, ts], kc == 0, kc == KD - 1)
                    for kc in range(KD):
                        MM(pb, WB[:, kc, :], src[:, kc, ts], kc == 0, kc == KD - 1)
                    TT("dve", T1[:, :], pa, ROPE[:, 0, ts], ALU.mult)
                    TT("dve", T2[:, :], pb, ROPE[:, 1, ts], ALU.mult)
                    TT("dve", dst_fn(tb), T1[:, :], T2[:, :], ALU.add)

            wq = w_q[0].rearrange("(kc p) n -> p kc n", p=128)
            wkv = w_kv.rearrange("(kc p) n -> p kc n", p=128)
            for h in range(8):
                load_w(WA, wq[:, :, h * 128:(h + 1) * 128], "wa")
                swap_halves(WB, WA)
                rope_proj(XN, lambda tb, h=h: QT[:, h, tb * NB:(tb + 1) * NB])

            T3 = Buf(aS, hS, 10240, F32, (NB,))
            T4 = Buf(aS, hS, 12288, F32, (NB,))
            T4B = Buf(aS, hS, 12288, BF16, (NB,))
            PT = [[Buf(aS, hS, 14336 + (2 * mp + b) * 1024, BF16, (NB,)) for b in range(2)] for mp in range(2)]

            for h in range(8):
                load_w(WA, wkv[:, :, h * 256:h * 256 + 128], "wa")
                swap_halves(WB, WA)
                load_w(WVb, wkv[:, :, h * 256 + 128:h * 256 + 256], "wv")
                rope_proj(XC, lambda tb: KT[:, tb * NB:(tb + 1) * NB])
                for t4 in range(4):
                    pv = PS(4 + t4 % 2)
                    for tt in range(4):
                        tok = slice((t4 * 4 + tt) * 128, (t4 * 4 + tt + 1) * 128)
                        for kc in range(KD):
                            MM(PS(4 + t4 % 2, tt * 128, (tt + 1) * 128), XC[:, kc, tok], WVb[:, kc, :], kc == 0, kc == KD - 1)
                    ACT(VV[:, t4 * NB:(t4 + 1) * NB], pv, AF.Copy)
                steps = [(qb, kt) for qb in range(NTB) for kt in range(4 * qb + 4)]

                def geom(qb, kt):
                    j = kt - 4 * qb
                    c0 = 128 * j if j > 0 else 0
                    return j, c0

                def s_stage(i):
                    qb, kt = steps[i]
                    j, c0 = geom(qb, kt)
                    qs0 = qb * NB
                    ks = slice(kt * 128, (kt + 1) * 128)
                    for mp in range(2):
                        ps_s = PS(2 * (i % 2) + mp, c0, NB)
                        p0, p1 = 64 * mp, 64 * mp + 64
                        MM(ps_s, KT[p0:p1, ks], QT[p0:p1, h, qs0 + c0:qs0 + NB], True, True)
                    for mp in range(2):
                        ps_s = PS(2 * (i % 2) + mp, c0, NB)
                        pt = PT[mp][i % 2]
                        ACT(pt[:, c0:NB], ps_s, AF.Exp, scale=0.125)
                        if j >= 0:
                            TT("pool", pt[:, c0:c0 + 128], pt[:, c0:c0 + 128], TRI, ALU.mult)

                def pv_stage(i):
                    qb, kt = steps[i]
                    j, c0 = geom(qb, kt)
                    ks = slice(kt * 128, (kt + 1) * 128)
                    first, last = kt == 0, kt == 4 * qb + 3
                    for mp in range(2):
                        pt = PT[mp][i % 2]
                        MM(PS(4 + mp, c0, NB), VV[:, ks], pt[:, c0:NB], first, last)
                        MM(PS(6 + mp, c0, NB), ONES, pt[:, c0:NB], first, last)

                def epi1(qb):
                    po1, po2, pl1, pl2 = PS(4), PS(5), PS(6), PS(7)
                    ACT(T3[:, :], pl1, AF.Ln)
                    ACT(T4[:, :], pl2, AF.Ln)
                    TS("dve", T1[:, :], po1, 1.0, None, ALU.mult)
                    TS("dve", T2[:, :], po2, 1.0, None, ALU.mult)
                    ACT(T3[:, :], T3[:, :], AF.Exp, scale=-1.0)
                    ACT(T4[:, :], T4[:, :], AF.Exp, scale=-1.0)

                def epi2(qb):
                    TT("dve", T1[:, :], T1[:, :], T3[:, :], ALU.mult)
                    STT("dve", T2[:, :], T2[:, :], dv("nlam"), T4[:, :], ALU.mult, ALU.mult)
                    TT("dve", T1[:, :], T1[:, :], T2[:, :], ALU.add)
                    TT("dve", T4B[:, :], T1[:, :], T1[:, :], ALU.mult)

                def epi3(qb, bank):
                    qs0 = qb * NB
                    pss = PS(bank)
                    MM(pss, ONESV, T4B[:, :], True, True)
                    ACT(T3[:, :], pss, AF.Ln, bias=EPS)
                    ACT(T3[:, :], T3[:, :], AF.Exp, scale=-0.5)
                    STT("dve", XN[:, h, qs0:qs0 + NB], T1[:, :], dv("gsub"), T3[:, :], ALU.mult, ALU.mult)

                s_stage(0)
                todo = []
                nst = len(steps)
                for i in range(nst):
                    if i + 1 < nst:
                        s_stage(i + 1)
                    pv_stage(i)
                    for (due, fn) in [t for t in todo if t[0] <= i]:
                        fn(i)
                    todo = [t for t in todo if t[0] > i]
                    qb, kt = steps[i]
                    if kt == 4 * qb + 3:
                        epi1(qb)
                        todo.append((i + 1, lambda j, qb=qb: epi2(qb)))
                        todo.append((i + 2, lambda j, qb=qb: epi3(qb, 2 * (j % 2))))
                for (due, fn) in todo:
                    fn(nst)

            WO = Buf(aC, hC, 0, BF16, (KD, D))
            r = WO[:, :, :]
            DMA("pool", r, w_o[0].rearrange("(kc p) n -> p kc n", p=128), "wo", writes=[r])
            for tb in range(NTB):
                for mo in range(KD):
                    ts = slice(tb * NB, (tb + 1) * NB)
                    po = PS(4 + cnt["po"] % 2)
                    cnt["po"] += 1
                    for hh in range(8):
                        MM(po, WO[:, hh, mo * 128:(mo + 1) * 128], XN[:, hh, ts], hh == 0, hh == 7)
                    TT("dve", HT[:, mo, ts], po, HT[:, mo, ts], ALU.add)
                if tail is not None:
                    tail.after_tb(tb)
            if tail is not None:
                tail.finish()

        dbg_ops = []

        def snapshot(i):
            if dbg_d is None:
                return
            v = dbg_d[i].rearrange("(c p) t -> p c t", p=128)
            dbg_ops.append(DMA("sp", v, HT[:, :, :], "dbg", reads=[HT[:, :, :]]))

        def norm_spec(st):
            if st == "ffn1_0":
                return [("ffn1_norm0", dst_of(XN))]
            if st == "rg":
                return [("mix_norm0", dst_of(XN))]
            if st == "ffn2_0":
                return [("ffn2_norm0", dst_of(XN))]
            if st == "ffn1_1":
                return [("ffn1_norm1", dst_of(XN)), ("kv_norm", dst_of(XC))]
            if st == "attn":
                g = [("mix_norm1", dst_of(XN))]
                if "ffn1_1" not in stages:
                    g.append(("kv_norm", dst_of(XC)))
                return g
            if st == "ffn2_1":
                return [("ffn2_norm1", dst_of(XN))]
            raise ValueError(st)

        OUTS = [Buf(aC, hC, i * 16384, F32, (KD, NB)) for i in range(2)]
        outT_v = outT.rearrange("(c p) t -> p c t", p=128)
        out_ops = []

        def out_post(tb):
            ts = slice(tb * NB, (tb + 1) * NB)
            ob = OUTS[tb % 2]
            out_ops.append(DMA("sp", outT_v[:, :, ts], ob[:, :, :], "out%d" % (tb % 2), reads=[ob[:, :, :]]))

        final_tail = NormTail([("final_norm", lambda c, tb: OUTS[tb % 2][:, c, :])], post=out_post)

        emit_norm(norm_spec(stages[0]))
        for si, st in enumerate(stages):
            last = si == len(stages) - 1
            use_tail = True
            if last:
                tail = final_tail if use_tail else None
            else:
                tail = NormTail(norm_spec(stages[si + 1])) if use_tail else None
            if st in ("ffn1_0", "ffn1_1"):
                emit_ffn("ffn1", int(st[-1]), tail)
            elif st in ("ffn2_0", "ffn2_1"):
                emit_ffn("ffn2", int(st[-1]), tail)
            elif st == "rg":
                emit_rg(tail)
            elif st == "attn":
                emit_attn(tail)
            snapshot(si)
            if not use_tail:
                if last:
                    for tb in range(NTB):
                        norm_n1(tb)
                        final_tail._n2(tb)
                else:
                    emit_norm(norm_spec(stages[si + 1]))

        fence = P.add("sp", None)
        fence.deps = list(out_ops) + list(dbg_ops)

        P.emit(nc)
    return nc


def _host_consts():
    pos = np.arange(T, dtype=np.float32)
    inv_freq = (10000.0 ** (-np.arange(0, 64, 2, dtype=np.float32) / 64.0)).astype(np.float32)
    ang = pos[None, :] * inv_freq[:, None]
    cos = np.cos(ang).astype(np.float32)
    sin = np.sin(ang).astype(np.float32)
    cos_t = np.concatenate([cos, cos, cos, cos], axis=0)
    sin_t = np.concatenate([-sin, sin, -sin, sin], axis=0)
    rope = np.concatenate([cos_t, sin_t], axis=1).astype(np.float32)
    c16 = np.zeros((128, 4, 128), dtype=np.float32)
    c16[:, 0, :] = 1.0
    c16[:, 1, :] = 1.0 / 1024.0
    c16[:, 2, :] = 1.0 / 128.0
    kk = np.arange(128)[:, None]
    qq = np.arange(128)[None, :]
    c16[:, 3, :] = (qq >= kk).astype(np.float32)
    return rope, c16.reshape(128, 512).astype(ml_dtypes.bfloat16)


def _pack_cvec(inp):
    cv = np.zeros((128, NCV), dtype=np.float32)

    def put(name, vec):
        vec = np.asarray(vec, dtype=np.float32).reshape(-1)
        n = vec.shape[0] // 128
        cv[:, CV[name]:CV[name] + n] = vec.reshape(n, 128).T

    put("ffn1_norm0", inp["ffn1_norm"][0]); put("ffn1_norm1", inp["ffn1_norm"][1])
    put("mix_norm0", inp["mix_norm"][0]); put("mix_norm1", inp["mix_norm"][1])
    put("ffn2_norm0", inp["ffn2_norm"][0]); put("ffn2_norm1", inp["ffn2_norm"][1])
    put("kv_norm", inp["kv_norm"]); put("final_norm", inp["final_norm"])
    put("rg_b_in", inp["rg_b_in"][0])
    put("rg_conv_w", inp["rg_conv_w"][0])
    put("rg_conv_b", inp["rg_conv_b"][0])
    put("rg_gate_b", inp["rg_gate_b"][0])
    put("rg_lambda", inp["rg_lambda"][0])
    put("rg_b_out", inp["rg_b_out"][0])
    put("subln", inp["diff_subln"][0])
    cv[:, CV["dlam"]:CV["dlam"] + 256] = np.asarray(inp["diff_lambda"][0], dtype=np.float32).reshape(1, 256)
    return cv


_CACHE = {}


def run(inputs, stages=ALL_STAGES, dbg=False, cores=8, trace=False):
    key = (tuple(stages), dbg)
    if key not in _CACHE:
        _CACHE[key] = build_program(stages, dbg)
    nc = _CACHE[key]
    rope, c16 = _host_consts()
    cvec = _pack_cvec(inputs)
    shared = {
        "cvec": cvec, "rope": rope, "c16": c16,
        "ffn1_w_in": np.ascontiguousarray(inputs["ffn1_w_in"], dtype=np.float32),
        "ffn1_w_out": np.ascontiguousarray(inputs["ffn1_w_out"], dtype=np.float32),
        "ffn2_w_in": np.ascontiguousarray(inputs["ffn2_w_in"], dtype=np.float32),
        "ffn2_w_out": np.ascontiguousarray(inputs["ffn2_w_out"], dtype=np.float32),
        "rg_w_in": np.ascontiguousarray(inputs["rg_w_in"], dtype=np.float32),
        "rg_gate_w": np.ascontiguousarray(inputs["rg_gate_w"], dtype=np.float32),
        "rg_w_out": np.ascontiguousarray(inputs["rg_w_out"], dtype=np.float32),
        "w_kv": np.ascontiguousarray(inputs["w_kv"], dtype=np.float32),
        "diff_w_q": np.ascontiguousarray(inputs["diff_w_q"], dtype=np.float32),
        "diff_w_o": np.ascontiguousarray(inputs["diff_w_o"], dtype=np.float32),
    }
    x = np.asarray(inputs["x"], dtype=np.float32)
    in_maps = []
    for b in range(cores):
        m = dict(shared)
        m["xT"] = np.ascontiguousarray(x[b].T)
        in_maps.append(m)
    res = run_bass_kernel_spmd(nc, in_maps, core_ids=list(range(cores)), trace=trace)
    return res


def kernel(**inputs):
    res = run(inputs)
    out = np.stack([np.asarray(r["outT"], dtype=np.float32).T for r in res.results], axis=0)
    return np.ascontiguousarray(out)
```
